# Optimizing a Trainium2 kernel written in Bass

```python
import math
import jax, jax.numpy as jnp
from jax import lax
import numpy as np

D_MODEL = 1024
BATCH = 8
SEQ = 2048
DEPTH = 1

MIX_WIDTH = D_MODEL
GDN_WIDTH = MIX_WIDTH // 2
GDN_HEAD_DIM = 64
GDN_HEADS = GDN_WIDTH // GDN_HEAD_DIM
GDN_CHUNK = 64
CONV_K = 5
SGU_WIDTH = MIX_WIDTH - GDN_WIDTH
SGU_GROUPS = 4
SGU_GROUP_DIM = SGU_WIDTH // SGU_GROUPS
SGU_CHUNK = 128
FF_DIM = 4 * D_MODEL
EPS = 1e-6

SPLIT_SIZES = (GDN_WIDTH, GDN_WIDTH, GDN_WIDTH, GDN_WIDTH,
               GDN_HEADS, GDN_HEADS, GDN_HEADS, GDN_HEADS,
               SGU_WIDTH, SGU_WIDTH)
SPLIT_POINTS = tuple(int(s) for s in np.cumsum(SPLIT_SIZES)[:-1])
IN_COLS = sum(SPLIT_SIZES)

kernel_name = "hybrid_gdn_sgu_encoder_block"


def rms_norm(x, gain):
    xf = x.astype(jnp.float32)
    y = xf * lax.rsqrt(jnp.mean(xf * xf, axis=-1, keepdims=True) + EPS)
    return (y * gain.astype(jnp.float32)).astype(x.dtype)


def l2_normalize(x):
    return x * lax.rsqrt(jnp.sum(x * x, axis=-1, keepdims=True) + EPS)


def short_conv(x, w):
    c = x.shape[-1]
    return lax.conv_general_dilated(
        x, w[:, None, :].astype(x.dtype), window_strides=(1,),
        padding=[(CONV_K // 2, CONV_K // 2)],
        dimension_numbers=("NWC", "WIO", "NWC"), feature_group_count=c)


def gated_delta_rule_chunked(q, k, v, g, beta):
    bsz, nh, s, dk = q.shape
    dv = v.shape[-1]
    n = s // GDN_CHUNK
    q = q * (dk ** -0.5)

    def chunks(t):
        return t.reshape(bsz, nh, n, GDN_CHUNK, *t.shape[3:])

    q, k, v, g, beta = chunks(q), chunks(k), chunks(v), chunks(g), chunks(beta)
    g = jnp.cumsum(g, axis=-1)
    tri_incl = jnp.tril(jnp.ones((GDN_CHUNK, GDN_CHUNK), dtype=bool))
    tri_strict = jnp.tril(jnp.ones((GDN_CHUNK, GDN_CHUNK), dtype=bool), -1)
    diff = g[..., :, None] - g[..., None, :]
    decay = jnp.exp(jnp.where(tri_incl, diff, -jnp.inf))

    k_beta = k * beta[..., None]
    v_beta = v * beta[..., None]
    lower = jnp.where(tri_strict, jnp.einsum("bhnid,bhnjd->bhnij", k_beta, k) * decay, 0.0)
    a_mat = jnp.eye(GDN_CHUNK, dtype=jnp.float32) + lower
    rhs = jnp.concatenate([v_beta, k_beta * jnp.exp(g)[..., None]], axis=-1)
    sol = lax.linalg.triangular_solve(a_mat, rhs, left_side=True, lower=True, unit_diagonal=True)
    u_c, w_c = sol[..., :dv], sol[..., dv:]

    attn_intra = jnp.einsum("bhnid,bhnjd->bhnij", q, k) * decay
    q_dec = q * jnp.exp(g)[..., None]
    k_dec = k * jnp.exp(g[..., -1:] - g)[..., None]
    g_last = jnp.exp(g[..., -1])

    def step(state, inp):
        u_i, w_i, q_i, k_i, a_i, gl = inp
        v_new = u_i - jnp.einsum("bhid,bhde->bhie", w_i, state)
        o_i = jnp.einsum("bhid,bhde->bhie", q_i, state) + jnp.einsum("bhij,bhje->bhie", a_i, v_new)
        state = state * gl[..., None, None] + jnp.einsum("bhid,bhie->bhde", k_i, v_new)
        return state, o_i

    xs = tuple(jnp.moveaxis(t, 2, 0) for t in (u_c, w_c, q_dec, k_dec, attn_intra, g_last))
    state0 = jnp.zeros((bsz, nh, dk, dv), jnp.float32)
    _, o = lax.scan(step, state0, xs)
    return jnp.moveaxis(o, 0, 2).reshape(bsz, nh, s, dv)


def bidir_gated_deltanet(qkv, z, b_f, b_b, a_f, a_b, a_log_f, dt_bias_f, a_log_b, dt_bias_b, norm_gain):
    bsz, s, _ = qkv.shape
    q, k, v = jnp.split(qkv.astype(jnp.float32), 3, axis=-1)

    def heads(t):
        return t.reshape(bsz, s, GDN_HEADS, GDN_HEAD_DIM).transpose(0, 2, 1, 3)

    q, k, v = l2_normalize(heads(q)), l2_normalize(heads(k)), heads(v)

    def gates(b, a, a_log, dt_bias):
        beta = jax.nn.sigmoid(b.astype(jnp.float32)).transpose(0, 2, 1)
        g = -jnp.exp(a_log.astype(jnp.float32)) * jax.nn.softplus(
            a.astype(jnp.float32) + dt_bias.astype(jnp.float32))
        return g.transpose(0, 2, 1), beta

    g_f, beta_f = gates(b_f, a_f, a_log_f, dt_bias_f)
    g_b, beta_b = gates(b_b, a_b, a_log_b, dt_bias_b)
    o_fwd = gated_delta_rule_chunked(q, k, v, g_f, beta_f)
    flip = lambda t: jnp.flip(t, axis=2)
    o_bwd = flip(gated_delta_rule_chunked(flip(q), flip(k), flip(v), flip(g_b), flip(beta_b)))
    o = (o_fwd + o_bwd).transpose(0, 2, 1, 3)
    zf = z.astype(jnp.float32).reshape(bsz, s, GDN_HEADS, GDN_HEAD_DIM)
    o = rms_norm(o, norm_gain) * jax.nn.silu(zf)
    return o.reshape(bsz, s, GDN_WIDTH).astype(z.dtype)


def spatial_gating(u, v, v_gain, w_s, b_s, out_gain):
    bsz, s, _ = u.shape
    u = jax.nn.gelu(u, approximate=False).reshape(bsz, s, SGU_GROUPS, SGU_GROUP_DIM)
    v = jax.nn.gelu(v, approximate=False).reshape(bsz, s, SGU_GROUPS, SGU_GROUP_DIM)
    v = rms_norm(v, v_gain).reshape(bsz, s // SGU_CHUNK, SGU_CHUNK, SGU_GROUPS, SGU_GROUP_DIM)
    mixed = jnp.einsum("gts,bnsgc->bntgc", w_s, v) + b_s.T[:, :, None]
    gated = u * mixed.reshape(bsz, s, SGU_GROUPS, SGU_GROUP_DIM)
    return rms_norm(gated, out_gain).reshape(bsz, s, SGU_WIDTH)


def setup_inputs(seed: int = 0) -> dict:
    key = jax.random.key(seed)
    ks = jax.random.split(key, 20)
    f32 = jnp.float32
    L = DEPTH
    x = jax.random.normal(ks[0], (BATCH, SEQ, D_MODEL), f32)
    norm_mix = 1.0 + 0.02 * jax.random.normal(ks[1], (L, D_MODEL), f32)
    w_in = jax.random.normal(ks[2], (L, D_MODEL, IN_COLS), f32) * D_MODEL ** -0.5
    conv_w = jax.random.normal(ks[3], (L, CONV_K, 3 * GDN_WIDTH), f32) * CONV_K ** -0.5

    def decay_params(k_a, k_dt):
        a_log = jnp.log(jax.random.uniform(k_a, (L, GDN_HEADS), f32, 1.0, 16.0))
        dt = jnp.exp(jax.random.uniform(k_dt, (L, GDN_HEADS), f32, math.log(1e-3), math.log(1e-1)))
        dt_bias = dt + jnp.log(-jnp.expm1(-dt))
        return a_log, dt_bias

    a_log_fwd, dt_bias_fwd = decay_params(ks[4], ks[5])
    a_log_bwd, dt_bias_bwd = decay_params(ks[6], ks[7])
    gdn_norm = 1.0 + 0.02 * jax.random.normal(ks[8], (L, GDN_HEAD_DIM), f32)
    sgu_v_norm = 1.0 + 0.02 * jax.random.normal(ks[9], (L, SGU_GROUPS, SGU_GROUP_DIM), f32)
    sgu_w = jax.random.normal(ks[10], (L, SGU_GROUPS, SGU_CHUNK, SGU_CHUNK), f32) * SGU_CHUNK ** -0.5
    sgu_b = 1.0 + 0.02 * jax.random.normal(ks[11], (L, SGU_GROUPS, SGU_CHUNK), f32)
    sgu_out_norm = 1.0 + 0.02 * jax.random.normal(ks[12], (L, SGU_GROUPS, SGU_GROUP_DIM), f32)
    w_out = jax.random.normal(ks[13], (L, MIX_WIDTH, D_MODEL), f32) * MIX_WIDTH ** -0.5
    norm_ffn = 1.0 + 0.02 * jax.random.normal(ks[14], (L, D_MODEL), f32)
    w_ff1 = jax.random.normal(ks[15], (L, D_MODEL, FF_DIM), f32) * D_MODEL ** -0.5
    w_ff2 = jax.random.normal(ks[16], (L, FF_DIM, D_MODEL), f32) * FF_DIM ** -0.5
    norm_final = 1.0 + 0.02 * jax.random.normal(ks[17], (D_MODEL,), f32)
    return {"x": x, "norm_mix": norm_mix, "w_in": w_in, "conv_w": conv_w,
            "a_log_fwd": a_log_fwd, "dt_bias_fwd": dt_bias_fwd,
            "a_log_bwd": a_log_bwd, "dt_bias_bwd": dt_bias_bwd,
            "gdn_norm": gdn_norm, "sgu_v_norm": sgu_v_norm, "sgu_w": sgu_w, "sgu_b": sgu_b,
            "sgu_out_norm": sgu_out_norm, "w_out": w_out, "norm_ffn": norm_ffn,
            "w_ff1": w_ff1, "w_ff2": w_ff2, "norm_final": norm_final}


def reference(x, norm_mix, w_in, conv_w, a_log_fwd, dt_bias_fwd, a_log_bwd, dt_bias_bwd,
              gdn_norm, sgu_v_norm, sgu_w, sgu_b, sgu_out_norm, w_out, norm_ffn,
              w_ff1, w_ff2, norm_final):
    for l in range(DEPTH):
        h = rms_norm(x, norm_mix[l])
        proj = jnp.einsum("bsd,dc->bsc", h, w_in[l])
        q, k, v, z, b_f, b_b, a_f, a_b, su, sv = jnp.split(proj, SPLIT_POINTS, axis=-1)
        qkv = jax.nn.silu(short_conv(jnp.concatenate([q, k, v], axis=-1), conv_w[l]))
        o_a = bidir_gated_deltanet(qkv, z, b_f, b_b, a_f, a_b, a_log_fwd[l], dt_bias_fwd[l],
                                   a_log_bwd[l], dt_bias_bwd[l], gdn_norm[l])
        o_b = spatial_gating(su, sv, sgu_v_norm[l], sgu_w[l], sgu_b[l], sgu_out_norm[l])
        mix = jnp.concatenate([o_a, o_b], axis=-1)
        x = x + jnp.einsum("bsc,cd->bsd", mix, w_out[l])
        h = rms_norm(x, norm_ffn[l])
        ff = jnp.square(jax.nn.relu(jnp.einsum("bsd,df->bsf", h, w_ff1[l])))
        x = x + jnp.einsum("bsf,fd->bsd", ff, w_ff2[l])
    return rms_norm(x, norm_final)
```

```python
import numpy as np
import concourse.bass as bass
import concourse.mybir as mybir
from concourse.bass_utils import run_bass_kernel_spmd

F32 = mybir.dt.float32
BF = mybir.dt.bfloat16
AF = mybir.ActivationFunctionType
ALU = mybir.AluOpType
AX = mybir.AxisListType

S = 2048
D = 1024
NT = 16
FF = 4096
INC = 3104
EPS = 1e-6
NDS = 8


class Prog:
    ENG = ("tensor", "vector", "scalar", "gpsimd", "sync")

    def __init__(self):
        self.ops = {e: [] for e in self.ENG}
        self.lastw = {}
        self.readers = {}
        self.rr = {e: 0 for e in self.ENG}
        self.dcnt = {}
        self.last_dma = {}

    def add(self, eng, fn, r=(), w=(), dma=False):
        idx = len(self.ops[eng])
        deps = set()
        for k in r:
            if k in self.lastw:
                deps.add(self.lastw[k])
            if isinstance(k, tuple) and k[0] == "ps":
                for rd in self.readers.get(k, ()):
                    if rd[0] != eng:
                        deps.add(rd)
        for k in w:
            if k in self.lastw:
                deps.add(self.lastw[k])
            for rd in self.readers.get(k, ()):
                deps.add(rd)
        op = dict(fn=fn, deps=deps, dma=None, sig=False, sv=0)
        if dma:
            key = (eng, self.rr[eng] % NDS)
            self.rr[eng] += 1
            c = self.dcnt.get(key, 0) + 1
            self.dcnt[key] = c
            op["dma"] = (key, c)
            self.last_dma[key] = (eng, idx)
        self.ops[eng].append(op)
        me = (eng, idx)
        for k in w:
            self.lastw[k] = me
            self.readers[k] = []
        for k in r:
            self.readers.setdefault(k, []).append(me)
        return me

    def barrier(self):
        lasts = []
        for e in self.ENG:
            for i in range(len(self.ops[e]) - 1, -1, -1):
                if self.ops[e][i]["fn"] is not None:
                    lasts.append((e, i))
                    break
        lasts = list(set(lasts) | set(self.last_dma.values()))
        for e in self.ENG:
            deps = set(x for x in lasts if x[0] != e or self.ops[x[0]][x[1]]["dma"] is not None)
            self.ops[e].append(dict(fn=None, deps=deps, dma=None, sig=False, sv=0))
        self.lastw = {}
        self.readers = {}

    def emit(self, nc, sems, dsems):
        ops = self.ops
        for e in self.ENG:
            for op in ops[e]:
                for (pe, pi) in op["deps"]:
                    p = ops[pe][pi]
                    if p["dma"] is None and not (pe == e == "tensor"):
                        p["sig"] = True
        for e in self.ENG:
            c = 0
            for op in ops[e]:
                if op["sig"]:
                    c += 1
                op["sv"] = c

        def run(e, h):
            waited = {}

            def wait(key, sem, val):
                if waited.get(key, 0) >= val:
                    return
                waited[key] = val
                h.wait_ge(sem, val)

            for op in ops[e]:
                for (pe, pi) in sorted(op["deps"]):
                    p = ops[pe][pi]
                    if p["dma"] is not None:
                        k, c = p["dma"]
                        wait(("d", k), dsems[k], 16 * c)
                    else:
                        if pe == e == "tensor":
                            continue
                        wait(("e", pe), sems[pe], p["sv"])
                if op["fn"] is None:
                    continue
                if op["dma"] is not None:
                    k, c = op["dma"]
                    if c > 1:
                        wait(("d", k), dsems[k], 16 * (c - 1))
                    op["fn"](h).then_inc(dsems[k], 16)
                else:
                    ins = op["fn"](h)
                    if op["sig"]:
                        ins.then_inc(sems[e], 1)
            for (k, c) in self.dcnt.items():
                if k[0] == e:
                    wait(("d", k), dsems[k], 16 * c)

        with nc.Block() as block:
            @block.tensor
            def _(h):
                run("tensor", h)

            @block.vector
            def _(h):
                run("vector", h)

            @block.scalar
            def _(h):
                run("scalar", h)

            @block.gpsimd
            def _(h):
                run("gpsimd", h)

            @block.sync
            def _(h):
                run("sync", h)


def build(stop=99):
    nc = bass.Bass("TRN2", target_bir_lowering=False)
    dt = lambda n, sh: nc.dram_tensor(n, sh, F32, kind="ExternalInput").ap()
    x = dt("x", [S, D])
    norm_mix = dt("norm_mix", [1, D])
    w_in = dt("w_in", [D, INC])
    conv_w = dt("conv_w", [5, 1536])
    a_log_fwd = dt("a_log_fwd", [1, 8])
    dt_bias_fwd = dt("dt_bias_fwd", [1, 8])
    a_log_bwd = dt("a_log_bwd", [1, 8])
    dt_bias_bwd = dt("dt_bias_bwd", [1, 8])
    gdn_norm = dt("gdn_norm", [1, 64])
    sgu_v_norm = dt("sgu_v_norm", [1, 512])
    sgu_w = dt("sgu_w", [4, 128, 128])
    sgu_b = dt("sgu_b", [4, 128])
    sgu_out_norm = dt("sgu_out_norm", [1, 512])
    w_out = dt("w_out", [D, D])
    norm_ffn = dt("norm_ffn", [1, D])
    w_ff1 = dt("w_ff1", [D, FF])
    w_ff2 = dt("w_ff2", [FF, D])
    norm_final = dt("norm_final", [1, D])
    y = nc.dram_tensor("y", [S, D], F32, kind="ExternalOutput").ap()

    NBYTES = 212800
    big = nc.alloc_sbuf_tensor("big", [128, NBYTES // 2], BF)
    ps = nc.alloc_psum_tensor("ps", [128, 8, 512], F32)

    class Arena:
        ptr = 0

    def V(shape, dtype):
        n = 1
        for s_ in shape[1:]:
            n *= s_
        nb = n * (4 if dtype == F32 else 2)
        nb = (nb + 31) // 32 * 32
        off = Arena.ptr
        Arena.ptr += nb
        assert Arena.ptr <= NBYTES, ("SBUF overflow", Arena.ptr)
        v = big[:, off // 2: off // 2 + (n * (4 if dtype == F32 else 2)) // 2]
        if dtype == F32:
            v = v.bitcast(F32)
        if len(shape) == 3:
            v = v.rearrange("p (a b) -> p a b", b=shape[2])
        elif len(shape) == 4:
            v = v.rearrange("p (a b c) -> p a b c", b=shape[2], c=shape[3])
        elif len(shape) == 5:
            v = v.rearrange("p (a b c d) -> p a b c d", b=shape[2], c=shape[3], d=shape[4])
        return v

    def psb(bank):
        return ps[:, bank, :].bitcast(BF)

    P = Prog()

    def mm(out, lhsT, rhs, start=True, stop=True, r=(), w=()):
        P.add("tensor", lambda e: e.matmul(out, lhsT=lhsT, rhs=rhs, start=start, stop=stop), r=r, w=w)

    def tr(out, in_, ident, r=(), w=()):
        P.add("tensor", lambda e: e.transpose(out, in_, ident), r=r, w=w)

    def act(out, in_, func, r=(), w=(), **kw):
        P.add("scalar", lambda e: e.activation(out=out, in_=in_, func=func, **kw), r=r, w=w)

    def tt(eng, out, in0, in1, op, r=(), w=()):
        P.add(eng, lambda e: e.tensor_tensor(out=out, in0=in0, in1=in1, op=op), r=r, w=w)

    def ts(eng, out, in0, s1, s2, op0, op1=None, r=(), w=()):
        if op1 is None:
            P.add(eng, lambda e: e.tensor_scalar(out=out, in0=in0, scalar1=s1, scalar2=None, op0=op0), r=r, w=w)
        else:
            P.add(eng, lambda e: e.tensor_scalar(out=out, in0=in0, scalar1=s1, scalar2=s2, op0=op0, op1=op1), r=r, w=w)

    def stt(eng, out, in0, scalar, in1, op0, op1, r=(), w=()):
        P.add(eng, lambda e: e.scalar_tensor_tensor(out=out, in0=in0, scalar=scalar, in1=in1, op0=op0, op1=op1), r=r, w=w)

    def cp(eng, out, in_, r=(), w=()):
        if eng == "scalar":
            P.add(eng, lambda e: e.activation(out=out, in_=in_, func=AF.Copy), r=r, w=w)
        else:
            P.add(eng, lambda e: e.tensor_copy(out=out, in_=in_), r=r, w=w)

    def dma(eng, out, in_, r=(), w=()):
        P.add(eng, lambda e: e.dma_start(out=out, in_=in_), r=r, w=w, dma=True)

    def red(out, in_, r=(), w=()):
        P.add("vector", lambda e: e.tensor_reduce(out=out, in_=in_, axis=AX.X, op=ALU.add), r=r, w=w)

    def memset(eng, ap, val, r=(), w=()):
        P.add(eng, lambda e: e.memset(ap, val), r=r, w=w)

    def bc_mid(ap2, n):
        return ap2.rearrange("p (o f) -> p o f", o=1).to_broadcast([ap2.shape[0], n, ap2.shape[1]])

    def bc_last(ap2, n):
        return ap2.rearrange("p (h o) -> p h o", o=1).to_broadcast([ap2.shape[0], ap2.shape[1], n])

    def recip(ap, key):
        P.add("vector", lambda e: e.reciprocal(out=ap, in_=ap), r=[key], w=[key])

    def rstd_from(ss, out, inv_n, key):
        act(out, ss, AF.Sqrt, r=[key + "_ss"], w=[key + "_r"], scale=inv_n, bias=EPS)
        recip(out, key + "_r")

    dif = V([128, 128], F32)
    ident_f = V([128, 128], F32)
    ident_bf = V([128, 128], BF)
    blockones = V([128, 128], BF)
    ones128 = V([128, 128], F32)
    d64 = V([128, 64], F32)
    ident64 = V([128, 64], F32)
    offdiag = V([128, 64], BF)
    M01 = V([128, 2, 64], F32)
    NM = V([128, 2, 64], F32)
    gmix = V([128, D], F32)
    cw = V([128, 12, 8], F32)
    sgub = V([128, 4], F32)
    WsT = V([128, 4, 128], BF)
    ABt = V([128, 16], F32)
    DTB = V([128, 16], F32)
    gdng = V([128, 64], F32)
    vgain = V([128, 512], F32)
    ogain = V([128, 512], F32)
    graw = V([128, 16, 32], F32)
    BETA = V([128, 16, 16], F32)
    NBETA = V([128, 16, 16], F32)
    GC = V([128, 16, 16], F32)
    EG = V([128, 16, 16], F32)
    EDEC = V([128, 16, 16], F32)
    GTall = V([128, 2, 16, 16], F32)
    GLs = V([128, 2, 16, 8], F32)
    smallt = V([128, 16], F32)
    H = V([128, 8, S], BF)
    H_off = Arena.ptr - 32768
    kq = V([128, 4, 2, S], BF)
    kT = kq[:, :, 0, :]
    qT = kq[:, :, 1, :]
    k_tok = V([128, 16, 512], BF)
    v_tok = V([128, 16, 512], BF)
    sz = V([128, 16, 512], BF)
    gu = V([128, 16, 512], BF)
    MARK = Arena.ptr

    dumps = {}

    def dump(name, ap):
        shp = list(ap.shape)
        d = nc.dram_tensor("dbg_" + name, shp, ap.dtype, kind="ExternalOutput").ap()
        dma("sync", d, ap)
        dumps[name] = shp

    def phases():
        P.add("gpsimd", lambda e: e.iota(dif, pattern=[[1, 128]], base=0, channel_multiplier=-1,
                                          allow_small_or_imprecise_dtypes=True), w=["dif"])
        ts("vector", ident_f, dif, 0.0, None, ALU.is_equal, r=["dif"], w=["ident_f"])
        cp("vector", ident_bf, ident_f, r=["ident_f"], w=["ident_bf"])
        memset("gpsimd", blockones, 0.0, w=["blockones"])
        memset("gpsimd", blockones[0:64, 0:64], 1.0, w=["blockones"])
        memset("gpsimd", blockones[64:128, 64:128], 1.0, w=["blockones"])
        memset("gpsimd", ones128, 1.0, w=["ones128"])
        cp("vector", d64[0:64, :], dif[0:64, 0:64], r=["dif"], w=["d64"])
        cp("vector", d64[64:128, :], dif[64:128, 64:128], r=["dif"], w=["d64"])
        ts("vector", ident64, d64, 0.0, None, ALU.is_equal, r=["d64"], w=["ident64"])
        ts("vector", offdiag, d64, 0.0, None, ALU.not_equal, r=["d64"], w=["offdiag"])
        ts("vector", M01[:, 0, :], d64, 0.0, None, ALU.is_ge, r=["d64"], w=["M01"])
        ts("vector", M01[:, 1, :], d64, 0.0, None, ALU.is_le, r=["d64"], w=["M01"])
        ts("vector", NM, M01, 1.0, 30000.0, ALU.subtract, ALU.mult, r=["M01"], w=["NM"])

        dma("sync", gmix, norm_mix.to_broadcast([128, D]), w=["gmix"])
        dma("sync", ABt[:, 0:8], a_log_fwd.to_broadcast([128, 8]), w=["ABt"])
        dma("sync", ABt[:, 8:16], a_log_bwd.to_broadcast([128, 8]), w=["ABt"])
        dma("sync", DTB[:, 0:8], dt_bias_fwd.to_broadcast([128, 8]), w=["DTB"])
        dma("sync", DTB[:, 8:16], dt_bias_bwd.to_broadcast([128, 8]), w=["DTB"])
        dma("sync", gdng, gdn_norm.to_broadcast([128, 64]), w=["gdng"])
        dma("sync", vgain, sgu_v_norm.to_broadcast([128, 512]), w=["vgain"])
        dma("sync", ogain, sgu_out_norm.to_broadcast([128, 512]), w=["ogain"])

        Arena.ptr = MARK
        xt = [V([128, D], F32) for _ in range(2)]
        xn = [V([128, D], BF) for _ in range(2)]
        junk = [V([128, D], BF) for _ in range(2)]
        ssx = [V([128, 4], F32) for _ in range(2)]
        cwsb = V([128, 1536], F32)
        sbsb = V([128, 128], F32)
        wsl = V([128, 4, 128], F32)

        dma("sync", cwsb[0:5, :], conv_w, w=["cwsb"])
        dma("sync", sbsb[0:4, :], sgu_b, w=["sbsb"])
        dma("sync", wsl, sgu_w.rearrange("g t s -> t g s"), w=["wsl"])
        for ct in range(12):
            mm(ps[:, 0, ct * 8: ct * 8 + 5], cwsb[0:5, ct * 128:(ct + 1) * 128], ident_f[0:5, 0:5],
               r=["cwsb", "ident_f"], w=[("ps", 0)])
        cp("vector", cw[:, :, 0:5], ps[:, 0, 0:96].rearrange("p (a b) -> p a b", b=8)[:, :, 0:5], r=[("ps", 0)], w=["cw"])
        mm(ps[:, 1, 0:4], sbsb[0:4, :], ident_f[0:4, 0:4], r=["sbsb", "ident_f"], w=[("ps", 1)])
        cp("vector", sgub, ps[:, 1, 0:4], r=[("ps", 1)], w=["sgub"])
        for g in range(4):
            mm(ps[:, 2, g * 128:(g + 1) * 128], wsl[:, g, :], ident_f, r=["wsl", "ident_f"], w=[("ps", 2)])
        cp("vector", WsT, ps[:, 2, :].rearrange("p (g t) -> p g t", t=128), r=[("ps", 2)], w=["WsT"])
        if stop == 0:
            P.barrier()
            for nm_, ap_ in [("ident_f", ident_f), ("ident64", ident64), ("M01", M01), ("NM", NM), ("cw", cw), ("sgub", sgub),
                             ("WsT", WsT), ("gmix", gmix), ("ABt", ABt), ("offdiag", offdiag), ("blockones", blockones)]:
                dump(nm_, ap_)
            return

        xv = x.rearrange("(t p) d -> t p d", p=128)
        def p1a_A(t):
            b = t % 2
            dma("sync" if t % 2 == 0 else "gpsimd", xt[b], xv[t], w=[("xt", b)])
            act(junk[b], xt[b], AF.Square, r=[("xt", b)], w=[("junk", b), "n1%d_ss" % b], accum_out=ssx[b][:, 0:1])
            rstd_from(ssx[b][:, 0:1], ssx[b][:, 1:2], 1.0 / D, "n1%d" % b)
            stt("vector", xn[b], xt[b], ssx[b][:, 1:2], gmix, ALU.mult, ALU.mult, r=[("xt", b), "n1%d_r" % b, "gmix"], w=[("xn", b)])

        def p1a_B(t):
            b = t % 2
            pb = 4 + (t % 2)
            for c in range(8):
                tr(psb(pb)[:, c * 128:(c + 1) * 128], xn[b][:, c * 128:(c + 1) * 128], ident_bf,
                   r=[("xn", b), "ident_bf"], w=[("ps", pb)])
            cp("scalar", H[:, :, t * 128:(t + 1) * 128], psb(pb).rearrange("p (c k) -> p c k", k=128),
               r=[("ps", pb)], w=[("hT", t // 4)])

        p1a_A(0)
        for t in range(NT):
            if t + 1 < NT:
                p1a_A(t + 1)
            p1a_B(t)
        P.barrier()
        if stop == 1:
            dump("hT", H)
            return

        Arena.ptr = MARK
        wbuf = [V([128, 8, 512], BF) for _ in range(3)]
        pre = V([128, 2052], BF)
        dgs = [V([128, 5, 128], BF) for _ in range(2)]
        sls = [V([128, S], BF) for _ in range(2)]
        rnts = [V([128, 512], F32) for _ in range(2)]
        sqs = [V([128, 512], BF) for _ in range(2)]
        gvs = [V([128, 512], F32) for _ in range(2)]
        tmp5s = [V([128, 512], BF) for _ in range(2)]
        vnbs = [V([128, 512], BF) for _ in range(2)]
        ss4s = [V([128, 8], F32) for _ in range(2)]

        memset("vector", pre[:, 0:2], 0.0, w=["prehalo"])
        memset("vector", pre[:, 2050:2052], 0.0, w=["prehalo"])
        wv = w_in.rearrange("(c p) n -> p c n", p=128)
        wcount = [0]

        def load_w(c0, nc_, slot=None):
            if slot is None:
                slot = wcount[0] % 2
                wcount[0] += 1
            dma("gpsimd", wbuf[slot][:, :, 0:nc_], wv[:, :, c0:c0 + nc_], w=[("wbuf", slot)])
            return slot

        hkeys = [("hT", s_) for s_ in range(4)]

        def tok_transposes(srcT, ct, dst, skey, dkey):
            for half in range(2):
                pb = 4 + half
                for j in range(8):
                    tti = half * 8 + j
                    tr(psb(pb)[:, j * 128:(j + 1) * 128], srcT[:, tti * 128:(tti + 1) * 128], ident_bf,
                       r=[skey, "ident_bf"], w=[("ps", pb)])
                cp("scalar", dst[:, half * 8:(half + 1) * 8, ct * 128:(ct + 1) * 128],
                   psb(pb).rearrange("p (j c) -> p j c", c=128), r=[("ps", pb)], w=[dkey])

        fm_count = [0]
        fm_pending = []

        def fm_A(kind, ct, slot):
            gct = {"q": 0, "k": 4, "v": 8}[kind] + ct
            sb = fm_count[0] % 2
            fm_count[0] += 1
            sl = sls[sb]
            dg = dgs[sb]
            for k in range(5):
                ts("vector", dg[:, k, :], ident_bf, cw[:, gct, k:k + 1], None, ALU.mult, r=["ident_bf", "cw"], w=[("dg", sb)])
            for s_ in range(4):
                for kc in range(8):
                    mm(ps[:, s_, :], wbuf[slot][:, kc, ct * 128:(ct + 1) * 128], H[:, kc, s_ * 512:(s_ + 1) * 512],
                       start=(kc == 0), stop=(kc == 7), r=[("wbuf", slot), hkeys[s_]], w=[("ps", s_)])
                cp("scalar" if s_ % 2 == 0 else "vector", pre[:, 2 + s_ * 512: 2 + (s_ + 1) * 512], ps[:, s_, :],
                   r=[("ps", s_)], w=[("pre", s_)])
            for s_ in range(4):
                for k in range(5):
                    mm(ps[:, s_, :], dg[:, k, :], pre[:, s_ * 512 + k: s_ * 512 + k + 512],
                       start=(k == 0), stop=(k == 4),
                       r=[("dg", sb), "prehalo"] + [("pre", j) for j in range(max(0, s_ - 1), min(4, s_ + 2))], w=[("ps", s_)])
                act(sl[:, s_ * 512:(s_ + 1) * 512], ps[:, s_, :], AF.Silu, r=[("ps", s_)], w=[("sl", sb)])
            return sb

        def fm_B(kind, ct, sb):
            sl = sls[sb]
            if kind == "v":
                tok_transposes(sl, ct, v_tok, ("sl", sb), "v_tok")
                return
            dst = qT if kind == "q" else kT
            for s_ in range(4):
                nb_ = 6 + (s_ % 2)
                q2 = s_ % 2
                tt("gpsimd", sqs[q2], sl[:, s_ * 512:(s_ + 1) * 512], sl[:, s_ * 512:(s_ + 1) * 512], ALU.mult,
                   r=[("sl", sb)], w=[("sq", q2)])
                mm(ps[:, nb_, :], blockones, sqs[q2], r=[("sq", q2), "blockones"], w=[("ps", nb_)])
                act(rnts[q2], ps[:, nb_, :], AF.Ln, r=[("ps", nb_)], w=[("rnt", q2)], bias=EPS)
                act(rnts[q2], rnts[q2], AF.Exp, r=[("rnt", q2)], w=[("rnt", q2)], scale=-0.5)
                stt("vector", dst[:, ct, s_ * 512:(s_ + 1) * 512], sl[:, s_ * 512:(s_ + 1) * 512],
                    0.125 if kind == "q" else 1.0, rnts[q2], ALU.mult, ALU.mult, r=[("sl", sb), ("rnt", q2)], w=[kind + "T"])
            if kind == "k":
                tok_transposes(kT[:, ct, :], ct, k_tok, "kT", "k_tok")

        def fm_group(kind, c0):
            slot = load_w(c0, 512)
            for ct in range(4):
                sb = fm_A(kind, ct, slot)
                tick_sv(1)
                while fm_pending:
                    fm_B(*fm_pending.pop(0))
                fm_pending.append((kind, ct, sb))
                tick_sv(1)

        def fm_flush():
            while fm_pending:
                fm_B(*fm_pending.pop(0))

        def tm_group(kind, c0, ncols):
            slot = load_w(c0, ncols)
            for t in range(NT):
                pb = 6 + (t % 2)
                for kc in range(8):
                    mm(ps[:, pb, 0:ncols], H[:, kc, t * 128:(t + 1) * 128], wbuf[slot][:, kc, 0:ncols],
                       start=(kc == 0), stop=(kc == 7), r=[("wbuf", slot), hkeys[t // 4]], w=[("ps", pb)])
                if kind == "z":
                    act(sz[:, t, :], ps[:, pb, :], AF.Silu, r=[("ps", pb)], w=[("sz", t)])
                    szv = sz[:, t, :].rearrange("p (h d) -> p h d", d=64)
                    tt("vector", szv, szv, bc_mid(gdng, 8), ALU.mult, r=[("sz", t), "gdng"], w=[("sz", t)])
                elif kind == "g":
                    cp("vector", graw[:, t, :], ps[:, pb, 0:32], r=[("ps", pb)], w=["graw"])
                elif kind == "su":
                    act(gu[:, t, :], ps[:, pb, :], AF.Gelu, r=[("ps", pb)], w=[("gu", t)])

        sv_pending = []
        sv_gens = []

        def sv_gen():
            slot = load_w(2592, 512, slot=2)
            for t in range(NT):
                pb = 6 + (t % 2)
                for kc in range(8):
                    mm(ps[:, pb, :], H[:, kc, t * 128:(t + 1) * 128], wbuf[slot][:, kc, :],
                       start=(kc == 0), stop=(kc == 7), r=[("wbuf", slot), hkeys[t // 4]], w=[("ps", pb)])
                sv_A(t, pb)
                while sv_pending:
                    sv_B(sv_pending.pop(0))
                sv_pending.append(t)
                yield
            while sv_pending:
                sv_B(sv_pending.pop(0))
            yield

        def tick_sv(n=1):
            for _ in range(n):
                for g_ in list(sv_gens):
                    try:
                        next(g_)
                    except StopIteration:
                        sv_gens.remove(g_)

        def sv_A(t, pb):
            b = t % 2
            gv, tmp5, vnb, ss4 = gvs[b], tmp5s[b], vnbs[b], ss4s[b]
            act(gv, ps[:, pb, :], AF.Gelu, r=[("ps", pb)], w=[("gv", b)])
            act(tmp5, gv, AF.Square, r=[("gv", b)], w=[("tmp5", b)])
            red(ss4[:, 0:4], tmp5.rearrange("p (g c) -> p g c", c=128), r=[("tmp5", b)], w=["sv%d_ss" % b])
            rstd_from(ss4[:, 0:4], ss4[:, 4:8], 1.0 / 128, "sv%d" % b)
            for g in range(4):
                gs = slice(g * 128, (g + 1) * 128)
                stt("vector", vnb[:, gs], gv[:, gs], ss4[:, 4 + g:5 + g], vgain[:, gs], ALU.mult, ALU.mult,
                    r=[("gv", b), "sv%d_r" % b, "vgain"], w=[("vnb", b)])

        def sv_B(t):
            b = t % 2
            gv, tmp5, vnb, ss4 = gvs[b], tmp5s[b], vnbs[b], ss4s[b]
            pb2 = 4 + (t % 2)
            for g in range(4):
                mm(ps[:, pb2, g * 128:(g + 1) * 128], WsT[:, g, :], vnb[:, g * 128:(g + 1) * 128],
                   r=["WsT", ("vnb", b)], w=[("ps", pb2)])
            for g in range(4):
                stt("vector", gv[:, g * 128:(g + 1) * 128], ps[:, pb2, g * 128:(g + 1) * 128], sgub[:, g:g + 1],
                    gu[:, t, g * 128:(g + 1) * 128], ALU.add, ALU.mult, r=[("ps", pb2), "sgub", ("gu", t)], w=[("gv", b)])
            act(tmp5, gv, AF.Square, r=[("gv", b)], w=[("tmp5", b)])
            red(ss4[:, 0:4], tmp5.rearrange("p (g c) -> p g c", c=128), r=[("tmp5", b)], w=["sv%d_ss" % b])
            rstd_from(ss4[:, 0:4], ss4[:, 4:8], 1.0 / 128, "sv%d" % b)
            for g in range(4):
                gs = slice(g * 128, (g + 1) * 128)
                stt("vector", gu[:, t, gs], gv[:, gs], ss4[:, 4 + g:5 + g], ogain[:, gs], ALU.mult, ALU.mult,
                    r=[("gv", b), "sv%d_r" % b, "ogain"], w=[("gu", t)])

        tm_group("g", 2048, 32)
        tm_group("su", 2080, 512)
        sv_gens.append(sv_gen())
        fm_group("k", 512)
        fm_group("q", 0)
        fm_group("v", 1024)
        fm_flush()
        while sv_gens:
            tick_sv(1)
        tm_group("z", 1536, 512)

        g2 = graw
        Gt = V([128, 16, 16], F32)
        T1 = V([128, 16, 16], F32)
        act(BETA, g2[:, :, 0:16], AF.Sigmoid, r=["graw"], w=["BETA"])
        ts("vector", NBETA, BETA, -1.0, None, ALU.mult, r=["BETA"], w=["NBETA"])
        tt("vector", T1, g2[:, :, 16:32], bc_mid(DTB, 16), ALU.add, r=["graw", "DTB"], w=["T1"])
        act(T1, T1, AF.Exp, r=["T1"], w=["T1"])
        act(T1, T1, AF.Ln, r=["T1"], w=["T1"], bias=1.0)
        act(smallt, ABt, AF.Exp, r=["ABt"], w=["smallt"])
        ts("vector", smallt, smallt, -1.0, None, ALU.mult, r=["smallt"], w=["smallt"])
        tt("vector", Gt, T1, bc_mid(smallt, 16), ALU.mult, r=["T1", "smallt"], w=["Gt"])
        for par in range(2):
            rows = slice(par * 64, (par + 1) * 64)
            for dr in range(2):
                mm(ps[rows, par, dr * 128:(dr + 1) * 128], M01[rows, dr, :], Gt[rows, :, dr * 8:(dr + 1) * 8],
                   r=["M01", "Gt"], w=[("ps", par)])
            mm(ps[:, 2 + par, 0:256], ones128[rows, :], Gt[rows, :, :], r=["ones128", "Gt"], w=[("ps", 2 + par)])
        for par in range(2):
            rows = slice(par * 64, (par + 1) * 64)
            for dr in range(2):
                cp("vector", GC[rows, :, dr * 8:(dr + 1) * 8],
                   ps[rows, par, dr * 128:(dr + 1) * 128].rearrange("p (t h) -> p t h", h=8), r=[("ps", par)], w=["GC"])
            cp("vector", GTall[:, par], ps[:, 2 + par, 0:256].rearrange("p (t h) -> p t h", h=16), r=[("ps", 2 + par)], w=["GTall"])
        act(EG, GC, AF.Exp, r=["GC"], w=["EG"])
        for par in range(2):
            rows = slice(par * 64, (par + 1) * 64)
            tt("vector", EDEC[rows], GTall[rows, par], GC[rows], ALU.subtract, r=["GTall", "GC"], w=["EDEC"])
        act(EDEC, EDEC, AF.Exp, r=["EDEC"], w=["EDEC"])
        gt6 = GTall.rearrange("p a t (dp o) -> p a t dp o", o=2)
        for odd in range(2):
            rows = slice(odd * 64, (odd + 1) * 64)
            act(GLs[rows], gt6[rows, :, :, :, odd], AF.Exp, r=["GTall"], w=["GLs"])
        P.barrier()
        if stop == 2:
            for nm_, ap_ in [("kT", kT), ("qT", qT), ("k_tok", k_tok), ("v_tok", v_tok), ("sz", sz), ("gu", gu), ("graw", graw),
                             ("BETA", BETA), ("GC", GC), ("EG", EG), ("EDEC", EDEC), ("GTall", GTall), ("GLs", GLs)]:
                dump(nm_, ap_)
            return

        Arena.ptr = MARK
        o_buf = V([128, 16, 512], BF)
        o32 = V([128, 8, 64], F32)
        p25a = V([128, 8, 64], F32)
        p25b = V([128, 512], BF)
        ss8 = V([128, 16], F32)
        mark2 = Arena.ptr
        region = {"use_h": False}
        Hptr = [H_off]

        def V2(shape, dtype):
            n = 1
            for s_ in shape[1:]:
                n *= s_
            nb = (n * (4 if dtype == F32 else 2) + 31) // 32 * 32
            if Arena.ptr + nb <= NBYTES:
                return V(shape, dtype)
            save = Arena.ptr
            Arena.ptr = Hptr[0]
            v = V(shape, dtype)
            Hptr[0] = Arena.ptr
            assert Hptr[0] <= H_off + 32768, "H overflow"
            Arena.ptr = save
            return v

        SH = [128, 8, 64]
        attnT = [[V2(SH, BF) for _ in range(2)] for _ in range(2)]
        Rw = [[V2(SH, BF) for _ in range(2)] for _ in range(2)]
        Ub = [[V2(SH, F32) for _ in range(2)] for _ in range(2)]
        kdec = [[V2(SH, BF) for _ in range(2)] for _ in range(2)]
        Dg = V2(SH, F32)
        dff = V2(SH, F32)
        dinc = V2(SH, BF)
        dstr = V2(SH, BF)
        tmpb = V2(SH, BF)
        nmg = V2(SH, F32)
        PXr = [V2([128, 8, 192], BF) for _ in range(2)]
        Pbd = [PXr[i][:, :, 0:128] for i in range(2)]
        Xbf = [PXr[i][:, :, 128:192] for i in range(2)]
        PTbd = [V2([128, 8, 128], BF) for _ in range(2)]
        X32 = [V2(SH, F32) for _ in range(2)]
        tmpx = V2([128, 8, 64], F32)
        ks_sb = [V2(SH, BF) for _ in range(2)]
        qs_sb = [V2(SH, F32) for _ in range(2)]
        vnew = [V2(SH, BF) for _ in range(2)]
        otmp = [V2(SH, F32) for _ in range(2)]
        Sg = [V2([128, 4, 128], F32) for _ in range(2)]
        Sst = [V2([128, 4, 128], BF) for _ in range(2)]

        memset("gpsimd", o_buf, 0.0, w=["o_buf"])
        for i_ in range(2):
            memset("gpsimd", PXr[i_], 0.0, w=[("P", i_, 0), ("P", i_, 1), ("Xbf", i_, 0), ("Xbf", i_, 1)])
        for dr in range(2):
            memset("vector", Sst[dr], 0.0, w=[("S", dr)])

        HALF = [slice(0, 64), slice(64, 128)]

        def v4(ap3, odd):
            return ap3.rearrange("p (pr o) i -> p pr o i", o=2)[:, :, odd, :]

        def s4(ap2, odd):
            return ap2.rearrange("p (pr o) -> p pr o", o=2)[:, :, odd]

        def pv(rows, bank):
            return ps[rows, bank, :].rearrange("p (h i) -> p h i", i=64)

        def prep(t, dr):
            lim = stop - 300 if 300 <= stop < 400 else 99
            sl_ = t % 2
            hd0 = dr * 8
            tok0 = t * 128
            K = lambda nm: (nm, dr, sl_)
            for h in range(8):
                hr = (h % 2) * 64
                pr = h // 2
                bk = h % 2
                for par in range(2):
                    rows = HALF[par]
                    tk = slice(tok0 + par * 64, tok0 + (par + 1) * 64)
                    mm(ps[rows, bk, pr * 128:(pr + 1) * 128], kT[hr:hr + 64, pr, tk], kq[hr:hr + 64, pr, :, tk], r=["kT", "qT"], w=[("ps", bk)])
            psG = lambda odd, kind: ps[:, odd, :].rearrange("p (pr k i) -> p pr k i", k=2, i=64)[:, :, kind, :]
            if lim <= 1:
                return
            gcs = GC[:, t, hd0:hd0 + 8]
            tt("vector", Dg, bc_mid(ident64, 8), bc_last(gcs, 64), ALU.mult, r=["ident64", "GC"], w=["Dg"])
            tt("vector", nmg, bc_mid(NM[:, dr, :], 8), bc_last(gcs, 64), ALU.subtract, r=["NM", "GC"], w=["nmg"])
            tt("vector", dstr, bc_mid(offdiag, 8), bc_last(NBETA[:, t, hd0:hd0 + 8], 64), ALU.mult, r=["offdiag", "NBETA"], w=["dstr"])
            for par in range(2):
                rows = HALF[par]
                mm(ps[rows, 2 + par, :], ones128[rows, 0:64], Dg[rows].rearrange("p h i -> p (h i)"), r=["ones128", "Dg"], w=[("ps", 2 + par)])
            for par in range(2):
                rows = HALF[par]
                tt("vector", dff[rows], pv(rows, 2 + par), nmg[rows], ALU.add, r=[("ps", 2 + par), "nmg"], w=["dff"])
            act(dinc, dff, AF.Exp, r=["dff"], w=["dinc"])
            tt("vector", tmpb, dinc, dstr, ALU.mult, r=["dinc", "dstr"], w=["tmpb"])
            if lim <= 4:
                return
            for odd in range(2):
                for par in range(2):
                    rows = HALF[par]
                    tt("vector", v4(Pbd[0][rows, :, par * 64:(par + 1) * 64], odd), psG(odd, 0)[rows], v4(tmpb[rows], odd), ALU.mult,
                       r=[("ps", odd), "tmpb"], w=[("P", 0, 0), ("P", 0, 1)])
                tt("vector", v4(attnT[dr][sl_], odd), psG(odd, 1), v4(dinc, odd), ALU.mult, r=[("ps", odd), "dinc"], w=[K("attnT")])
            tt("vector", X32[0], bc_mid(ident64, 8), bc_last(BETA[:, t, hd0:hd0 + 8], 64), ALU.mult,
               r=["ident64", "BETA"], w=[("X32", 0, 0), ("X32", 0, 1)])
            cp("gpsimd", Xbf[0], X32[0], r=[("X32", 0, 0), ("X32", 0, 1)], w=[("Xbf", 0, 0), ("Xbf", 0, 1)])
            if lim <= 5:
                return
            tick()
            HK = [0, 1]
            for h in range(8):
                mm(ps[:, h // 4, (h % 4) * 128:(h % 4 + 1) * 128], Pbd[0][:, h, :], ident_bf,
                   r=[("P", 0, h // 4), "ident_bf"], w=[("ps", h // 4)])
            for hf in range(2):
                cp("scalar", PTbd[0][:, hf * 4:(hf + 1) * 4, :],
                   ps[:, hf, :].rearrange("p (s c) -> p s c", c=128), r=[("ps", hf)], w=[("PT", 0, hf)])
            for k in range(6):
                a, b_ = k % 2, (k + 1) % 2
                for h in range(8):
                    mm(ps[:, 4, h * 64:(h + 1) * 64], PTbd[a][:, h, :], Xbf[a][:, h, :],
                       r=[("PT", a, h // 4), ("Xbf", a, 0), ("Xbf", a, 1)], w=[("ps", 4)])
                tt("vector", Xbf[b_], ps[:, 4, :].rearrange("p (h i) -> p h i", i=64), X32[a], ALU.add,
                   r=[("X32", a, 0), ("X32", a, 1), ("ps", 4)], w=[("Xbf", b_, 0), ("Xbf", b_, 1)])
                if k < 5:
                    cp("scalar", tmpx, ps[:, 4, :].rearrange("p (h i) -> p h i", i=64), r=[("ps", 4)], w=["tmpx"])
                if k < 5:
                    for h in range(8):
                        mm(ps[:, h // 4, (h % 4) * 128:(h % 4 + 1) * 128], Pbd[a][:, h, :], PTbd[a][:, h, :],
                           r=[("P", a, h // 4), ("PT", a, h // 4)], w=[("ps", h // 4)])
                    for hf in range(2):
                        cp("scalar", PTbd[b_][:, hf * 4:(hf + 1) * 4, :],
                           ps[:, hf, :].rearrange("p (s c) -> p s c", c=128), r=[("ps", hf)], w=[("PT", b_, hf)])
                tick()
                if k < 4:
                    for h in range(8):
                        mm(ps[:, 2 + h // 4, (h % 4) * 128:(h % 4 + 1) * 128], PTbd[a][:, h, :], Pbd[a][:, h, :],
                           r=[("P", a, h // 4), ("PT", a, h // 4)], w=[("ps", 2 + h // 4)])
                    for hf in range(2):
                        cp("scalar", Pbd[b_][:, hf * 4:(hf + 1) * 4, :],
                           ps[:, 2 + hf, :].rearrange("p (s c) -> p s c", c=128), r=[("ps", 2 + hf)], w=[("P", b_, hf)])
                if k < 5:
                    tt("gpsimd", X32[b_], tmpx, X32[a], ALU.add, r=[("X32", a, 0), ("X32", a, 1), "tmpx"],
                       w=[("X32", b_, 0), ("X32", b_, 1)])
            XK = [("Xbf", 0, 0), ("Xbf", 0, 1)]
            tt("gpsimd", Rw[dr][sl_], Xbf[0], bc_last(EG[:, t, hd0:hd0 + 8], 64), ALU.mult, r=XK + ["EG"], w=[K("Rw")])
            for par in range(2):
                rows = HALF[par]
                for h in range(8):
                    mm(ps[rows, 2 + par, h * 64:(h + 1) * 64], Xbf[0][rows, h, :], v_tok[rows, t, h * 64:(h + 1) * 64],
                       r=XK + ["v_tok"], w=[("ps", 2 + par)])
            for par in range(2):
                rows = HALF[par]
                cp("scalar" if par == 0 else "vector", Ub[dr][sl_][rows], pv(rows, 2 + par), r=[("ps", 2 + par)], w=[K("Ub")])
            tt("gpsimd", kdec[dr][sl_], k_tok[:, t, :].rearrange("p (h d) -> p h d", d=64),
               bc_last(EDEC[:, t, hd0:hd0 + 8], 64), ALU.mult, r=["k_tok", "EDEC"], w=[K("kdec")])

        pending = []

        def tick():
            for g in list(pending):
                try:
                    next(g)
                except StopIteration:
                    pending.remove(g)

        def scan_dir(dr, chunks):
            for c in chunks:
                yield from scan(c, dr)

        def scan(c, dr):
            t = c // 2
            par = c % 2
            sl_ = t % 2
            hd0 = dr * 8
            rows = HALF[par]
            tk = slice(c * 64, (c + 1) * 64)
            K = lambda nm: (nm, dr, sl_)
            bW = 5
            bS = 5
            for pr in range(4):
                mm(ps[rows, 6, pr * 128:(pr + 1) * 128], kT[:, pr, tk], Sst[dr][:, pr, :], r=["kT", ("S", dr)], w=[("ps", 6)])
            cp("scalar", ks_sb[dr][rows], pv(rows, 6), r=[("ps", 6)], w=[("ks", dr)])
            yield
            for pr in range(4):
                mm(ps[rows, 7, pr * 128:(pr + 1) * 128], qT[:, pr, tk], Sst[dr][:, pr, :], r=["qT", ("S", dr)], w=[("ps", 7)])
            tt("vector", qs_sb[dr][rows], pv(rows, 7), bc_last(EG[rows, t, hd0:hd0 + 8], 64), ALU.mult,
               r=[("ps", 7), "EG"], w=[("qs", dr)])
            yield
            for h in range(8):
                mm(ps[rows, bW, h * 64:(h + 1) * 64], Rw[dr][sl_][rows, h, :], ks_sb[dr][rows, h, :], r=[K("Rw"), ("ks", dr)], w=[("ps", bW)])
            tt("vector", vnew[dr][rows], Ub[dr][sl_][rows], pv(rows, bW), ALU.subtract, r=[K("Ub"), ("ps", bW)], w=[("vnew", dr)])
            yield
            for h in range(8):
                mm(ps[rows, bW, h * 64:(h + 1) * 64], attnT[dr][sl_][rows, h, :], vnew[dr][rows, h, :], r=[K("attnT"), ("vnew", dr)], w=[("ps", bW)])
            tt("vector", otmp[dr][rows], pv(rows, bW), qs_sb[dr][rows], ALU.add, r=[("ps", bW), ("qs", dr)], w=[("otmp", dr)])
            yield
            for pr in range(4):
                mm(ps[:, bS, pr * 128:(pr + 1) * 128],
                   kdec[dr][sl_][rows, 2 * pr:2 * pr + 2, :].rearrange("p a d -> p (a d)"),
                   vnew[dr][rows, 2 * pr:2 * pr + 2, :].rearrange("p a e -> p (a e)"),
                   r=[K("kdec"), ("vnew", dr)], w=[("ps", bS)])
            ob = o_buf[rows, t, :].rearrange("p (h d) -> p h d", d=64)
            tt("gpsimd", ob, ob, otmp[dr][rows], ALU.add, r=["o_buf", ("otmp", dr)], w=["o_buf"])
            tt("gpsimd", Sg[dr], Sst[dr], bc_last(GLs[:, par, t, dr * 4:(dr + 1) * 4], 128), ALU.mult, r=[("S", dr), "GLs"], w=[("Sg", dr)])
            psS = ps[:, bS, :].rearrange("p (a e) -> p a e", e=128)
            for hh in range(2):
                rh = HALF[hh]
                cs = slice(hh * 64, (hh + 1) * 64)
                tt("vector", Sst[dr][rh, :, cs], Sg[dr][rh, :, cs], psS[rh, :, cs], ALU.add,
                   r=[("Sg", dr), ("ps", bS)], w=[("S", dr)])
            yield

        prep(0, 0)
        if stop == 30 or 300 <= stop < 400:
            P.barrier()
            for nm_, ap_ in [("attnT", attnT[0][0]), ("Rw", Rw[0][0]), ("Ub", Ub[0][0]), ("kdec", kdec[0][0]), ("Rb", Xbf[0]),
                             ("dinc", dinc), ("dstr", dstr), ("dff", dff), ("Dg", Dg)]:
                dump(nm_, ap_)
            return
        prep(15, 1)
        for rnd in range(16):
            pending.append(scan_dir(0, [2 * rnd, 2 * rnd + 1]))
            pending.append(scan_dir(1, [31 - 2 * rnd, 30 - 2 * rnd]))
            tick()
            if rnd < 15:
                prep(rnd + 1, 0)
                prep(14 - rnd, 1)
            while pending:
                tick()
        P.barrier()
        if stop == 3:
            dump("o_buf", o_buf)
            return

        mixT = H
        Arena.ptr = H_off + 32768
        x1 = V([128, NT, D], F32)
        for t in range(NT):
            dma("gpsimd", x1[:, t, :], xv[t], w=[("x1", t)])
        Arena.ptr = mark2
        o32s = [V([128, 8, 64], F32) for _ in range(2)]
        p25as = [V([128, 8, 64], F32) for _ in range(2)]
        p25bs = [V([128, 512], BF) for _ in range(2)]
        ss8s = [V([128, 16], F32) for _ in range(2)]
        def p25_A(t):
            b = t % 2
            o32, p25a, p25b, ss8 = o32s[b], p25as[b], p25bs[b], ss8s[b]
            ov = o_buf[:, t, :].rearrange("p (h d) -> p h d", d=64)
            act(o32, ov, AF.Square, r=["o_buf"], w=[("o32", b)])
            red(ss8[:, 0:8], o32, r=[("o32", b)], w=["gn%d_ss" % b])
            rstd_from(ss8[:, 0:8], ss8[:, 8:16], 1.0 / 64, "gn%d" % b)
            tt("vector", p25a, ov, bc_last(ss8[:, 8:16], 64), ALU.mult, r=["o_buf", "gn%d_r" % b], w=[("p25a", b)])
            tt("vector", p25b, p25a.rearrange("p h d -> p (h d)"), sz[:, t, :], ALU.mult, r=[("p25a", b), "sz"], w=[("p25b", b)])

        def p25_B(t):
            b = t % 2
            p25b = p25bs[b]
            pb = 4 + (t % 2)
            for c in range(4):
                tr(psb(pb)[:, c * 128:(c + 1) * 128], p25b[:, c * 128:(c + 1) * 128], ident_bf, r=[("p25b", b), "ident_bf"], w=[("ps", pb)])
            for c in range(4):
                tr(psb(pb)[:, (4 + c) * 128:(5 + c) * 128], gu[:, t, c * 128:(c + 1) * 128], ident_bf,
                   r=[("gu", t), "ident_bf"], w=[("ps", pb)])
            cp("scalar", mixT[:, :, t * 128:(t + 1) * 128], psb(pb).rearrange("p (c k) -> p c k", k=128),
               r=[("ps", pb)], w=[("mixT", t // 4)])

        p25_A(0)
        for t in range(NT):
            if t + 1 < NT:
                p25_A(t + 1)
            p25_B(t)
        P.barrier()
        if stop == 4:
            dump("mixT", mixT)
            return

        Arena.ptr = H_off + 32768 + NT * D * 4
        ring = [V([128, 8, 512], BF) for _ in range(2)]
        stage = [V([128, 8, 512], F32) for _ in range(2)]
        ffT = V([128, 4, S], BF)
        rt = [V([128, 512], BF) for _ in range(2)]
        xn2 = [V([128, D], BF) for _ in range(2)]
        junk3 = [V([128, D], BF) for _ in range(2)]
        ss3b = V([128, 4], F32)
        outt = [V([128, D], F32) for _ in range(2)]
        gvec = V([128, D], F32)
        ss3 = V([128, 4], F32)
        h2T = H

        dma("sync", gvec, norm_ffn.to_broadcast([128, D]), w=["gvec"])
        wov = w_out.rearrange("(c p) n -> p c n", p=128)
        w1v = w_ff1.rearrange("(c p) n -> p c n", p=128)
        w2v = w_ff2.rearrange("(c p) n -> p c n", p=128)
        yv = y.rearrange("(t p) d -> t p d", p=128)
        rcount = [0]

        def stream(src_ap, as_w2=False):
            sl_ = rcount[0] % 2
            rcount[0] += 1
            dst = stage[sl_]
            if as_w2:
                dst = dst.rearrange("p a b -> p (a b)").rearrange("p (f n) -> p f n", n=D)
            dma("sync", dst, src_ap, w=[("stage", sl_)])
            cp("vector", ring[sl_], stage[sl_], r=[("stage", sl_)], w=[("ring", sl_)])
            return sl_

        wo_slots = [stream(wov[:, :, hf * 512:(hf + 1) * 512]) for hf in range(2)]
        def p3a_A(t):
            for nh in range(2):
                for ct in range(8):
                    mm(ps[:, nh, :], mixT[:, ct, t * 128:(t + 1) * 128], ring[wo_slots[nh]][:, ct, :],
                       start=(ct == 0), stop=(ct == 7), r=[("HT", t), ("ring", wo_slots[nh])], w=[("ps", nh)])
                tt("vector", x1[:, t, nh * 512:(nh + 1) * 512], ps[:, nh, :], x1[:, t, nh * 512:(nh + 1) * 512], ALU.add,
                   r=[("ps", nh), ("x1", t)], w=[("x1", t)])
            b = t % 2
            sst = ss3 if b == 0 else ss3b
            act(junk3[b], x1[:, t, :], AF.Square, r=[("x1", t)], w=[("junk3", b), "n2%d_ss" % b], accum_out=sst[:, 0:1])
            rstd_from(sst[:, 0:1], sst[:, 1:2], 1.0 / D, "n2%d" % b)
            stt("vector", xn2[b], x1[:, t, :], sst[:, 1:2], gvec, ALU.mult, ALU.mult, r=[("x1", t), "n2%d_r" % b, "gvec"], w=[("xn2", b)])

        def p3a_B(t):
            b = t % 2
            pb = 2 + (t % 2)
            for c in range(8):
                tr(psb(pb)[:, c * 128:(c + 1) * 128], xn2[b][:, c * 128:(c + 1) * 128], ident_bf, r=[("xn2", b), "ident_bf"], w=[("ps", pb)])
            cp("scalar", h2T[:, :, t * 128:(t + 1) * 128], psb(pb).rearrange("p (c k) -> p c k", k=128),
               r=[("ps", pb)], w=[("HT", t)])

        p3a_A(0)
        for t in range(NT):
            if t + 1 < NT:
                p3a_A(t + 1)
            p3a_B(t)
        dma("sync", gvec, norm_final.to_broadcast([128, D]), r=["gvec"], w=["gvec"])
        def final_tile(t):
            ob_ = t % 2
            sst = ss3 if ob_ == 0 else ss3b
            act(junk3[ob_], x1[:, t, :], AF.Square, r=[("x1", t)], w=[("junk3", ob_), "n3%d_ss" % ob_], accum_out=sst[:, 2:3])
            rstd_from(sst[:, 2:3], sst[:, 3:4], 1.0 / D, "n3%d" % ob_)
            stt("vector", outt[ob_], x1[:, t, :], sst[:, 3:4], gvec, ALU.mult, ALU.mult,
                r=[("x1", t), "n3%d_r" % ob_, "gvec"], w=[("outt", ob_)])
            dma("gpsimd" if t % 2 == 0 else "sync", yv[t], outt[ob_], r=[("outt", ob_)])


        HTK = [("HT", t) for t in range(NT)]
        ecount = 0
        for g in range(8):
            s1 = stream(w1v[:, :, g * 512:(g + 1) * 512])
            for f4 in range(4):
                for sp in range(4):
                    pb = 4 + (ecount % 4)
                    for kc in range(8):
                        mm(ps[:, pb, :], ring[s1][:, kc, f4 * 128:(f4 + 1) * 128], h2T[:, kc, sp * 512:(sp + 1) * 512],
                           start=(kc == 0), stop=(kc == 7), r=[("ring", s1)] + HTK[sp * 4:(sp + 1) * 4], w=[("ps", pb)])
                    rb = ecount % 2
                    act(rt[rb], ps[:, pb, :], AF.Relu, r=[("ps", pb)], w=[("rt", rb)])
                    tt("vector" if rb == 0 else "gpsimd", ffT[:, f4, sp * 512:(sp + 1) * 512], rt[rb], rt[rb], ALU.mult,
                       r=[("rt", rb)], w=[("ffT", f4, sp)])
                    ecount += 1
            s2 = stream(w2v[:, g * 4:(g + 1) * 4, :], as_w2=True)
            w2s = ring[s2].rearrange("p a b -> p (a b)").rearrange("p (f n) -> p f n", n=D)
            for t in range(NT):
                for nh in range(2):
                    pb = (t * 2 + nh) % 4
                    for f4 in range(4):
                        mm(ps[:, pb, :], ffT[:, f4, t * 128:(t + 1) * 128], w2s[:, f4, nh * 512:(nh + 1) * 512],
                           start=(f4 == 0), stop=(f4 == 3), r=[("ffT", f4, t // 4), ("ring", s2)], w=[("ps", pb)])
                    tt("vector", x1[:, t, nh * 512:(nh + 1) * 512], ps[:, pb, :], x1[:, t, nh * 512:(nh + 1) * 512], ALU.add,
                       r=[("ps", pb), ("x1", t)], w=[("x1", t)])
                if g == 7 and t >= 1:
                    final_tile(t - 1)
            if g == 7:
                final_tile(NT - 1)

    phases()

    import contextlib
    with contextlib.ExitStack() as st:
        sems = {e: st.enter_context(nc.semaphore("s_" + e)) for e in Prog.ENG}
        dsems = {}
        for e in ("sync", "gpsimd"):
            for i in range(NDS):
                dsems[(e, i)] = st.enter_context(nc.semaphore("d_%s_%d" % (e, i)))
        P.emit(nc, sems, dsems)
    nc._dumps = dumps
    return nc


_NC = None


def kernel(**inputs):
    global _NC
    if _NC is None:
        _NC = build()
    nc = _NC
    names = ["norm_mix", "w_in", "conv_w", "a_log_fwd", "dt_bias_fwd", "a_log_bwd", "dt_bias_bwd", "gdn_norm",
             "sgu_v_norm", "sgu_w", "sgu_b", "sgu_out_norm", "w_out", "norm_ffn", "w_ff1", "w_ff2", "norm_final"]
    shp = {"norm_mix": (1, D), "w_in": (D, INC), "conv_w": (5, 1536), "a_log_fwd": (1, 8), "dt_bias_fwd": (1, 8),
           "a_log_bwd": (1, 8), "dt_bias_bwd": (1, 8), "gdn_norm": (1, 64), "sgu_v_norm": (1, 512),
           "sgu_w": (4, 128, 128), "sgu_b": (4, 128), "sgu_out_norm": (1, 512), "w_out": (D, D), "norm_ffn": (1, D),
           "w_ff1": (D, FF), "w_ff2": (FF, D), "norm_final": (1, D)}
    shared = {n: np.ascontiguousarray(np.asarray(inputs[n], dtype=np.float32).reshape(shp[n])) for n in names}
    xin = np.asarray(inputs["x"], dtype=np.float32)
    in_maps = []
    for c in range(8):
        m = dict(shared)
        m["x"] = np.ascontiguousarray(xin[c])
        in_maps.append(m)
    res = run_bass_kernel_spmd(nc, in_maps, core_ids=list(range(8)))
    return np.stack([np.asarray(res.results[c]["y"], dtype=np.float32) for c in range(8)], axis=0)
```

```python
import numpy as np
import concourse.bass as bass
import concourse.mybir as mybir
from concourse.bass_utils import run_bass_kernel_spmd

F32 = mybir.dt.float32
BF = mybir.dt.bfloat16
AF = mybir.ActivationFunctionType
ALU = mybir.AluOpType
AX = mybir.AxisListType

S = 2048
D = 1024
NT = 16
FF = 4096
INC = 3104
EPS = 1e-6
NDS = 8


class Prog:
    ENG = ("tensor", "vector", "scalar", "gpsimd", "sync")

    def __init__(self):
        self.ops = {e: [] for e in self.ENG}
        self.lastw = {}
        self.readers = {}
        self.rr = {e: 0 for e in self.ENG}
        self.dcnt = {}
        self.last_dma = {}

    def add(self, eng, fn, r=(), w=(), dma=False):
        idx = len(self.ops[eng])
        deps = set()
        for k in r:
            if k in self.lastw:
                deps.add(self.lastw[k])
            if isinstance(k, tuple) and k[0] == "ps":
                for rd in self.readers.get(k, ()):
                    if rd[0] != eng:
                        deps.add(rd)
        for k in w:
            if k in self.lastw:
                deps.add(self.lastw[k])
            for rd in self.readers.get(k, ()):
                deps.add(rd)
        op = dict(fn=fn, deps=deps, dma=None, sig=False, sv=0)
        if dma:
            key = (eng, self.rr[eng] % NDS)
            self.rr[eng] += 1
            c = self.dcnt.get(key, 0) + 1
            self.dcnt[key] = c
            op["dma"] = (key, c)
            self.last_dma[key] = (eng, idx)
        self.ops[eng].append(op)
        me = (eng, idx)
        for k in w:
            self.lastw[k] = me
            self.readers[k] = []
        for k in r:
            self.readers.setdefault(k, []).append(me)
        return me

    def barrier(self):
        lasts = []
        for e in self.ENG:
            for i in range(len(self.ops[e]) - 1, -1, -1):
                if self.ops[e][i]["fn"] is not None:
                    lasts.append((e, i))
                    break
        lasts = list(set(lasts) | set(self.last_dma.values()))
        for e in self.ENG:
            deps = set(x for x in lasts if x[0] != e or self.ops[x[0]][x[1]]["dma"] is not None)
            self.ops[e].append(dict(fn=None, deps=deps, dma=None, sig=False, sv=0))
        self.lastw = {}
        self.readers = {}

    def emit(self, nc, sems, dsems):
        ops = self.ops
        for e in self.ENG:
            for op in ops[e]:
                for (pe, pi) in op["deps"]:
                    p = ops[pe][pi]
                    if p["dma"] is None and not (pe == e == "tensor"):
                        p["sig"] = True
        for e in self.ENG:
            c = 0
            for op in ops[e]:
                if op["sig"]:
                    c += 1
                op["sv"] = c

        def run(e, h):
            waited = {}

            def wait(key, sem, val):
                if waited.get(key, 0) >= val:
                    return
                waited[key] = val
                h.wait_ge(sem, val)

            for op in ops[e]:
                for (pe, pi) in sorted(op["deps"]):
                    p = ops[pe][pi]
                    if p["dma"] is not None:
                        k, c = p["dma"]
                        wait(("d", k), dsems[k], 16 * c)
                    else:
                        if pe == e == "tensor":
                            continue
                        wait(("e", pe), sems[pe], p["sv"])
                if op["fn"] is None:
                    continue
                if op["dma"] is not None:
                    k, c = op["dma"]
                    if c > 1:
                        wait(("d", k), dsems[k], 16 * (c - 1))
                    op["fn"](h).then_inc(dsems[k], 16)
                else:
                    ins = op["fn"](h)
                    if op["sig"]:
                        ins.then_inc(sems[e], 1)
            for (k, c) in self.dcnt.items():
                if k[0] == e:
                    wait(("d", k), dsems[k], 16 * c)

        with nc.Block() as block:
            @block.tensor
            def _(h):
                run("tensor", h)

            @block.vector
            def _(h):
                run("vector", h)

            @block.scalar
            def _(h):
                run("scalar", h)

            @block.gpsimd
            def _(h):
                run("gpsimd", h)

            @block.sync
            def _(h):
                run("sync", h)


def build(stop=99):
    nc = bass.Bass("TRN2", target_bir_lowering=False)
    dt = lambda n, sh: nc.dram_tensor(n, sh, F32, kind="ExternalInput").ap()
    x = dt("x", [S, D])
    norm_mix = dt("norm_mix", [1, D])
    w_in = dt("w_in", [D, INC])
    conv_w = dt("conv_w", [5, 1536])
    a_log_fwd = dt("a_log_fwd", [1, 8])
    dt_bias_fwd = dt("dt_bias_fwd", [1, 8])
    a_log_bwd = dt("a_log_bwd", [1, 8])
    dt_bias_bwd = dt("dt_bias_bwd", [1, 8])
    gdn_norm = dt("gdn_norm", [1, 64])
    sgu_v_norm = dt("sgu_v_norm", [1, 512])
    sgu_w = dt("sgu_w", [4, 128, 128])
    sgu_b = dt("sgu_b", [4, 128])
    sgu_out_norm = dt("sgu_out_norm", [1, 512])
    w_out = dt("w_out", [D, D])
    norm_ffn = dt("norm_ffn", [1, D])
    w_ff1 = dt("w_ff1", [D, FF])
    w_ff2 = dt("w_ff2", [FF, D])
    norm_final = dt("norm_final", [1, D])
    y = nc.dram_tensor("y", [S, D], F32, kind="ExternalOutput").ap()

    NBYTES = 212800
    big = nc.alloc_sbuf_tensor("big", [128, NBYTES // 2], BF)
    ps = nc.alloc_psum_tensor("ps", [128, 8, 512], F32)

    class Arena:
        ptr = 0

    def V(shape, dtype):
        n = 1
        for s_ in shape[1:]:
            n *= s_
        nb = n * (4 if dtype == F32 else 2)
        nb = (nb + 31) // 32 * 32
        off = Arena.ptr
        Arena.ptr += nb
        assert Arena.ptr <= NBYTES, ("SBUF overflow", Arena.ptr)
        v = big[:, off // 2: off // 2 + (n * (4 if dtype == F32 else 2)) // 2]
        if dtype == F32:
            v = v.bitcast(F32)
        if len(shape) == 3:
            v = v.rearrange("p (a b) -> p a b", b=shape[2])
        elif len(shape) == 4:
            v = v.rearrange("p (a b c) -> p a b c", b=shape[2], c=shape[3])
        elif len(shape) == 5:
            v = v.rearrange("p (a b c d) -> p a b c d", b=shape[2], c=shape[3], d=shape[4])
        return v

    def psb(bank):
        return ps[:, bank, :].bitcast(BF)

    P = Prog()

    def mm(out, lhsT, rhs, start=True, stop=True, r=(), w=()):
        P.add("tensor", lambda e: e.matmul(out, lhsT=lhsT, rhs=rhs, start=start, stop=stop), r=r, w=w)

    def tr(out, in_, ident, r=(), w=()):
        P.add("tensor", lambda e: e.transpose(out, in_, ident), r=r, w=w)

    def act(out, in_, func, r=(), w=(), **kw):
        P.add("scalar", lambda e: e.activation(out=out, in_=in_, func=func, **kw), r=r, w=w)

    def tt(eng, out, in0, in1, op, r=(), w=()):
        P.add(eng, lambda e: e.tensor_tensor(out=out, in0=in0, in1=in1, op=op), r=r, w=w)

    def ts(eng, out, in0, s1, s2, op0, op1=None, r=(), w=()):
        if op1 is None:
            P.add(eng, lambda e: e.tensor_scalar(out=out, in0=in0, scalar1=s1, scalar2=None, op0=op0), r=r, w=w)
        else:
            P.add(eng, lambda e: e.tensor_scalar(out=out, in0=in0, scalar1=s1, scalar2=s2, op0=op0, op1=op1), r=r, w=w)

    def stt(eng, out, in0, scalar, in1, op0, op1, r=(), w=()):
        P.add(eng, lambda e: e.scalar_tensor_tensor(out=out, in0=in0, scalar=scalar, in1=in1, op0=op0, op1=op1), r=r, w=w)

    def cp(eng, out, in_, r=(), w=()):
        if eng == "scalar":
            P.add(eng, lambda e: e.activation(out=out, in_=in_, func=AF.Copy), r=r, w=w)
        else:
            P.add(eng, lambda e: e.tensor_copy(out=out, in_=in_), r=r, w=w)

    def dma(eng, out, in_, r=(), w=()):
        P.add(eng, lambda e: e.dma_start(out=out, in_=in_), r=r, w=w, dma=True)

    def red(out, in_, r=(), w=()):
        P.add("vector", lambda e: e.tensor_reduce(out=out, in_=in_, axis=AX.X, op=ALU.add), r=r, w=w)

    def memset(eng, ap, val, r=(), w=()):
        P.add(eng, lambda e: e.memset(ap, val), r=r, w=w)

    def bc_mid(ap2, n):
        return ap2.rearrange("p (o f) -> p o f", o=1).to_broadcast([ap2.shape[0], n, ap2.shape[1]])

    def bc_last(ap2, n):
        return ap2.rearrange("p (h o) -> p h o", o=1).to_broadcast([ap2.shape[0], ap2.shape[1], n])

    def recip(ap, key):
        P.add("vector", lambda e: e.reciprocal(out=ap, in_=ap), r=[key], w=[key])

    def rstd_from(ss, out, inv_n, key):
        act(out, ss, AF.Sqrt, r=[key + "_ss"], w=[key + "_r"], scale=inv_n, bias=EPS)
        recip(out, key + "_r")

    dif = V([128, 128], F32)
    ident_f = V([128, 128], F32)
    ident_bf = V([128, 128], BF)
    blockones = V([128, 128], BF)
    ones128 = V([128, 128], F32)
    d64 = V([128, 64], F32)
    ident64 = V([128, 64], F32)
    offdiag = V([128, 64], BF)
    M01 = V([128, 2, 64], F32)
    NM = V([128, 2, 64], F32)
    gmix = V([128, D], F32)
    cw = V([128, 12, 8], F32)
    sgub = V([128, 4], F32)
    WsT = V([128, 4, 128], BF)
    ABt = V([128, 16], F32)
    DTB = V([128, 16], F32)
    gdng = V([128, 64], F32)
    vgain = V([128, 512], F32)
    ogain = V([128, 512], F32)
    graw = V([128, 16, 32], F32)
    BETA = V([128, 16, 16], F32)
    NBETA = V([128, 16, 16], F32)
    GC = V([128, 16, 16], F32)
    EG = V([128, 16, 16], F32)
    EDEC = V([128, 16, 16], F32)
    GTall = V([128, 2, 16, 16], F32)
    GLs = V([128, 2, 16, 8], F32)
    smallt = V([128, 16], F32)
    H = V([128, 8, S], BF)
    H_off = Arena.ptr - 32768
    kq = V([128, 4, 2, S], BF)
    kT = kq[:, :, 0, :]
    qT = kq[:, :, 1, :]
    k_tok = V([128, 16, 512], BF)
    v_tok = V([128, 16, 512], BF)
    sz = V([128, 16, 512], BF)
    gu = V([128, 16, 512], BF)
    MARK = Arena.ptr

    dumps = {}

    def dump(name, ap):
        shp = list(ap.shape)
        d = nc.dram_tensor("dbg_" + name, shp, ap.dtype, kind="ExternalOutput").ap()
        dma("sync", d, ap)
        dumps[name] = shp

    def phases():
        P.add("gpsimd", lambda e: e.iota(dif, pattern=[[1, 128]], base=0, channel_multiplier=-1,
                                          allow_small_or_imprecise_dtypes=True), w=["dif"])
        ts("vector", ident_f, dif, 0.0, None, ALU.is_equal, r=["dif"], w=["ident_f"])
        cp("vector", ident_bf, ident_f, r=["ident_f"], w=["ident_bf"])
        memset("gpsimd", blockones, 0.0, w=["blockones"])
        memset("gpsimd", blockones[0:64, 0:64], 1.0, w=["blockones"])
        memset("gpsimd", blockones[64:128, 64:128], 1.0, w=["blockones"])
        memset("gpsimd", ones128, 1.0, w=["ones128"])
        cp("vector", d64[0:64, :], dif[0:64, 0:64], r=["dif"], w=["d64"])
        cp("vector", d64[64:128, :], dif[64:128, 64:128], r=["dif"], w=["d64"])
        ts("vector", ident64, d64, 0.0, None, ALU.is_equal, r=["d64"], w=["ident64"])
        ts("vector", offdiag, d64, 0.0, None, ALU.not_equal, r=["d64"], w=["offdiag"])
        ts("vector", M01[:, 0, :], d64, 0.0, None, ALU.is_ge, r=["d64"], w=["M01"])
        ts("vector", M01[:, 1, :], d64, 0.0, None, ALU.is_le, r=["d64"], w=["M01"])
        ts("vector", NM, M01, 1.0, 30000.0, ALU.subtract, ALU.mult, r=["M01"], w=["NM"])

        dma("sync", gmix, norm_mix.to_broadcast([128, D]), w=["gmix"])
        dma("sync", ABt[:, 0:8], a_log_fwd.to_broadcast([128, 8]), w=["ABt"])
        dma("sync", ABt[:, 8:16], a_log_bwd.to_broadcast([128, 8]), w=["ABt"])
        dma("sync", DTB[:, 0:8], dt_bias_fwd.to_broadcast([128, 8]), w=["DTB"])
        dma("sync", DTB[:, 8:16], dt_bias_bwd.to_broadcast([128, 8]), w=["DTB"])
        dma("sync", gdng, gdn_norm.to_broadcast([128, 64]), w=["gdng"])
        dma("sync", vgain, sgu_v_norm.to_broadcast([128, 512]), w=["vgain"])
        dma("sync", ogain, sgu_out_norm.to_broadcast([128, 512]), w=["ogain"])

        Arena.ptr = MARK
        xt = [V([128, D], F32) for _ in range(2)]
        xn = [V([128, D], BF) for _ in range(2)]
        junk = [V([128, D], BF) for _ in range(2)]
        ssx = [V([128, 4], F32) for _ in range(2)]
        cwsb = V([128, 1536], F32)
        sbsb = V([128, 128], F32)
        wsl = V([128, 4, 128], F32)

        dma("sync", cwsb[0:5, :], conv_w, w=["cwsb"])
        dma("sync", sbsb[0:4, :], sgu_b, w=["sbsb"])
        dma("sync", wsl, sgu_w.rearrange("g t s -> t g s"), w=["wsl"])
        for ct in range(12):
            mm(ps[:, 0, ct * 8: ct * 8 + 5], cwsb[0:5, ct * 128:(ct + 1) * 128], ident_f[0:5, 0:5],
               r=["cwsb", "ident_f"], w=[("ps", 0)])
        cp("vector", cw[:, :, 0:5], ps[:, 0, 0:96].rearrange("p (a b) -> p a b", b=8)[:, :, 0:5], r=[("ps", 0)], w=["cw"])
        mm(ps[:, 1, 0:4], sbsb[0:4, :], ident_f[0:4, 0:4], r=["sbsb", "ident_f"], w=[("ps", 1)])
        cp("vector", sgub, ps[:, 1, 0:4], r=[("ps", 1)], w=["sgub"])
        for g in range(4):
            mm(ps[:, 2, g * 128:(g + 1) * 128], wsl[:, g, :], ident_f, r=["wsl", "ident_f"], w=[("ps", 2)])
        cp("vector", WsT, ps[:, 2, :].rearrange("p (g t) -> p g t", t=128), r=[("ps", 2)], w=["WsT"])
        if stop == 0:
            P.barrier()
            for nm_, ap_ in [("ident_f", ident_f), ("ident64", ident64), ("M01", M01), ("NM", NM), ("cw", cw), ("sgub", sgub),
                             ("WsT", WsT), ("gmix", gmix), ("ABt", ABt), ("offdiag", offdiag), ("blockones", blockones)]:
                dump(nm_, ap_)
            return

        xv = x.rearrange("(t p) d -> t p d", p=128)
        def p1a_A(t):
            b = t % 2
            dma("sync", xt[b], xv[t], w=[("xt", b)])
            act(junk[b], xt[b], AF.Square, r=[("xt", b)], w=[("junk", b), "n1%d_ss" % b], accum_out=ssx[b][:, 0:1])
            rstd_from(ssx[b][:, 0:1], ssx[b][:, 1:2], 1.0 / D, "n1%d" % b)
            stt("vector", xn[b], xt[b], ssx[b][:, 1:2], gmix, ALU.mult, ALU.mult, r=[("xt", b), "n1%d_r" % b, "gmix"], w=[("xn", b)])

        def p1a_B(t):
            b = t % 2
            pb = 4 + (t % 2)
            for c in range(8):
                tr(psb(pb)[:, c * 128:(c + 1) * 128], xn[b][:, c * 128:(c + 1) * 128], ident_bf,
                   r=[("xn", b), "ident_bf"], w=[("ps", pb)])
            cp("scalar", H[:, :, t * 128:(t + 1) * 128], psb(pb).rearrange("p (c k) -> p c k", k=128),
               r=[("ps", pb)], w=[("hT", t // 4)])

        p1a_A(0)
        for t in range(NT):
            if t + 1 < NT:
                p1a_A(t + 1)
            p1a_B(t)
        P.barrier()
        if stop == 1:
            dump("hT", H)
            return

        Arena.ptr = MARK
        wbuf = [V([128, 8, 512], BF) for _ in range(3)]
        pre = V([128, 2052], BF)
        dgs = [V([128, 5, 128], BF) for _ in range(2)]
        sls = [V([128, S], BF) for _ in range(2)]
        rnts = [V([128, 512], F32) for _ in range(2)]
        sqs = [V([128, 512], BF) for _ in range(2)]
        gvs = [V([128, 512], F32) for _ in range(2)]
        tmp5s = [V([128, 512], BF) for _ in range(2)]
        vnbs = [V([128, 512], BF) for _ in range(2)]
        ss4s = [V([128, 8], F32) for _ in range(2)]

        memset("vector", pre[:, 0:2], 0.0, w=["prehalo"])
        memset("vector", pre[:, 2050:2052], 0.0, w=["prehalo"])
        wv = w_in.rearrange("(c p) n -> p c n", p=128)
        wcount = [0]

        def load_w(c0, nc_, slot=None):
            if slot is None:
                slot = wcount[0] % 2
                wcount[0] += 1
            dma("gpsimd", wbuf[slot][:, :, 0:nc_], wv[:, :, c0:c0 + nc_], w=[("wbuf", slot)])
            return slot

        hkeys = [("hT", s_) for s_ in range(4)]

        def tok_transposes(srcT, ct, dst, skey, dkey):
            for half in range(2):
                pb = 4 + half
                for j in range(8):
                    tti = half * 8 + j
                    tr(psb(pb)[:, j * 128:(j + 1) * 128], srcT[:, tti * 128:(tti + 1) * 128], ident_bf,
                       r=[skey, "ident_bf"], w=[("ps", pb)])
                cp("scalar", dst[:, half * 8:(half + 1) * 8, ct * 128:(ct + 1) * 128],
                   psb(pb).rearrange("p (j c) -> p j c", c=128), r=[("ps", pb)], w=[dkey])

        fm_count = [0]
        fm_pending = []

        def fm_A(kind, ct, slot):
            gct = {"q": 0, "k": 4, "v": 8}[kind] + ct
            sb = fm_count[0] % 2
            fm_count[0] += 1
            sl = sls[sb]
            dg = dgs[sb]
            for k in range(5):
                ts("vector", dg[:, k, :], ident_bf, cw[:, gct, k:k + 1], None, ALU.mult, r=["ident_bf", "cw"], w=[("dg", sb)])
            for s_ in range(4):
                for kc in range(8):
                    mm(ps[:, s_, :], wbuf[slot][:, kc, ct * 128:(ct + 1) * 128], H[:, kc, s_ * 512:(s_ + 1) * 512],
                       start=(kc == 0), stop=(kc == 7), r=[("wbuf", slot), hkeys[s_]], w=[("ps", s_)])
                cp("scalar" if s_ % 2 == 0 else "vector", pre[:, 2 + s_ * 512: 2 + (s_ + 1) * 512], ps[:, s_, :],
                   r=[("ps", s_)], w=[("pre", s_)])
            for s_ in range(4):
                for k in range(5):
                    mm(ps[:, s_, :], dg[:, k, :], pre[:, s_ * 512 + k: s_ * 512 + k + 512],
                       start=(k == 0), stop=(k == 4),
                       r=[("dg", sb), "prehalo"] + [("pre", j) for j in range(max(0, s_ - 1), min(4, s_ + 2))], w=[("ps", s_)])
                act(sl[:, s_ * 512:(s_ + 1) * 512], ps[:, s_, :], AF.Silu, r=[("ps", s_)], w=[("sl", sb)])
            return sb

        def fm_B(kind, ct, sb):
            sl = sls[sb]
            if kind == "v":
                tok_transposes(sl, ct, v_tok, ("sl", sb), "v_tok")
                return
            dst = qT if kind == "q" else kT
            for s_ in range(4):
                nb_ = 6 + (s_ % 2)
                q2 = s_ % 2
                tt("gpsimd", sqs[q2], sl[:, s_ * 512:(s_ + 1) * 512], sl[:, s_ * 512:(s_ + 1) * 512], ALU.mult,
                   r=[("sl", sb)], w=[("sq", q2)])
                mm(ps[:, nb_, :], blockones, sqs[q2], r=[("sq", q2), "blockones"], w=[("ps", nb_)])
                act(rnts[q2], ps[:, nb_, :], AF.Ln, r=[("ps", nb_)], w=[("rnt", q2)], bias=EPS)
                act(rnts[q2], rnts[q2], AF.Exp, r=[("rnt", q2)], w=[("rnt", q2)], scale=-0.5)
                stt("vector", dst[:, ct, s_ * 512:(s_ + 1) * 512], sl[:, s_ * 512:(s_ + 1) * 512],
                    0.125 if kind == "q" else 1.0, rnts[q2], ALU.mult, ALU.mult, r=[("sl", sb), ("rnt", q2)], w=[kind + "T"])
            if kind == "k":
                tok_transposes(kT[:, ct, :], ct, k_tok, "kT", "k_tok")

        def fm_group(kind, c0):
            slot = load_w(c0, 512)
            for ct in range(4):
                sb = fm_A(kind, ct, slot)
                tick_sv(1)
                while fm_pending:
                    fm_B(*fm_pending.pop(0))
                fm_pending.append((kind, ct, sb))
                tick_sv(1)

        def fm_flush():
            while fm_pending:
                fm_B(*fm_pending.pop(0))

        def tm_group(kind, c0, ncols):
            slot = load_w(c0, ncols)
            for t in range(NT):
                pb = 6 + (t % 2)
                for kc in range(8):
                    mm(ps[:, pb, 0:ncols], H[:, kc, t * 128:(t + 1) * 128], wbuf[slot][:, kc, 0:ncols],
                       start=(kc == 0), stop=(kc == 7), r=[("wbuf", slot), hkeys[t // 4]], w=[("ps", pb)])
                if kind == "z":
                    act(sz[:, t, :], ps[:, pb, :], AF.Silu, r=[("ps", pb)], w=[("sz", t)])
                    szv = sz[:, t, :].rearrange("p (h d) -> p h d", d=64)
                    tt("vector", szv, szv, bc_mid(gdng, 8), ALU.mult, r=[("sz", t), "gdng"], w=[("sz", t)])
                elif kind == "g":
                    cp("vector", graw[:, t, :], ps[:, pb, 0:32], r=[("ps", pb)], w=["graw"])
                elif kind == "su":
                    act(gu[:, t, :], ps[:, pb, :], AF.Gelu, r=[("ps", pb)], w=[("gu", t)])

        sv_pending = []
        sv_gens = []

        def sv_gen():
            slot = load_w(2592, 512, slot=2)
            for t in range(NT):
                pb = 6 + (t % 2)
                for kc in range(8):
                    mm(ps[:, pb, :], H[:, kc, t * 128:(t + 1) * 128], wbuf[slot][:, kc, :],
                       start=(kc == 0), stop=(kc == 7), r=[("wbuf", slot), hkeys[t // 4]], w=[("ps", pb)])
                sv_A(t, pb)
                while sv_pending:
                    sv_B(sv_pending.pop(0))
                sv_pending.append(t)
                yield
            while sv_pending:
                sv_B(sv_pending.pop(0))
            yield

        def tick_sv(n=1):
            for _ in range(n):
                for g_ in list(sv_gens):
                    try:
                        next(g_)
                    except StopIteration:
                        sv_gens.remove(g_)

        def sv_A(t, pb):
            b = t % 2
            gv, tmp5, vnb, ss4 = gvs[b], tmp5s[b], vnbs[b], ss4s[b]
            act(gv, ps[:, pb, :], AF.Gelu, r=[("ps", pb)], w=[("gv", b)])
            act(tmp5, gv, AF.Square, r=[("gv", b)], w=[("tmp5", b)])
            red(ss4[:, 0:4], tmp5.rearrange("p (g c) -> p g c", c=128), r=[("tmp5", b)], w=["sv%d_ss" % b])
            rstd_from(ss4[:, 0:4], ss4[:, 4:8], 1.0 / 128, "sv%d" % b)
            for g in range(4):
                gs = slice(g * 128, (g + 1) * 128)
                stt("vector", vnb[:, gs], gv[:, gs], ss4[:, 4 + g:5 + g], vgain[:, gs], ALU.mult, ALU.mult,
                    r=[("gv", b), "sv%d_r" % b, "vgain"], w=[("vnb", b)])

        def sv_B(t):
            b = t % 2
            gv, tmp5, vnb, ss4 = gvs[b], tmp5s[b], vnbs[b], ss4s[b]
            pb2 = 4 + (t % 2)
            for g in range(4):
                mm(ps[:, pb2, g * 128:(g + 1) * 128], WsT[:, g, :], vnb[:, g * 128:(g + 1) * 128],
                   r=["WsT", ("vnb", b)], w=[("ps", pb2)])
            for g in range(4):
                stt("vector", gv[:, g * 128:(g + 1) * 128], ps[:, pb2, g * 128:(g + 1) * 128], sgub[:, g:g + 1],
                    gu[:, t, g * 128:(g + 1) * 128], ALU.add, ALU.mult, r=[("ps", pb2), "sgub", ("gu", t)], w=[("gv", b)])
            act(tmp5, gv, AF.Square, r=[("gv", b)], w=[("tmp5", b)])
            red(ss4[:, 0:4], tmp5.rearrange("p (g c) -> p g c", c=128), r=[("tmp5", b)], w=["sv%d_ss" % b])
            rstd_from(ss4[:, 0:4], ss4[:, 4:8], 1.0 / 128, "sv%d" % b)
            for g in range(4):
                gs = slice(g * 128, (g + 1) * 128)
                stt("vector", gu[:, t, gs], gv[:, gs], ss4[:, 4 + g:5 + g], ogain[:, gs], ALU.mult, ALU.mult,
                    r=[("gv", b), "sv%d_r" % b, "ogain"], w=[("gu", t)])

        tm_group("g", 2048, 32)
        tm_group("su", 2080, 512)
        sv_gens.append(sv_gen())
        fm_group("k", 512)
        fm_group("q", 0)
        fm_group("v", 1024)
        fm_flush()
        while sv_gens:
            tick_sv(1)
        tm_group("z", 1536, 512)

        g2 = graw
        Gt = V([128, 16, 16], F32)
        T1 = V([128, 16, 16], F32)
        act(BETA, g2[:, :, 0:16], AF.Sigmoid, r=["graw"], w=["BETA"])
        ts("vector", NBETA, BETA, -1.0, None, ALU.mult, r=["BETA"], w=["NBETA"])
        tt("vector", T1, g2[:, :, 16:32], bc_mid(DTB, 16), ALU.add, r=["graw", "DTB"], w=["T1"])
        act(T1, T1, AF.Exp, r=["T1"], w=["T1"])
        act(T1, T1, AF.Ln, r=["T1"], w=["T1"], bias=1.0)
        act(smallt, ABt, AF.Exp, r=["ABt"], w=["smallt"])
        ts("vector", smallt, smallt, -1.0, None, ALU.mult, r=["smallt"], w=["smallt"])
        tt("vector", Gt, T1, bc_mid(smallt, 16), ALU.mult, r=["T1", "smallt"], w=["Gt"])
        for par in range(2):
            rows = slice(par * 64, (par + 1) * 64)
            for dr in range(2):
                mm(ps[rows, par, dr * 128:(dr + 1) * 128], M01[rows, dr, :], Gt[rows, :, dr * 8:(dr + 1) * 8],
                   r=["M01", "Gt"], w=[("ps", par)])
            mm(ps[:, 2 + par, 0:256], ones128[rows, :], Gt[rows, :, :], r=["ones128", "Gt"], w=[("ps", 2 + par)])
        for par in range(2):
            rows = slice(par * 64, (par + 1) * 64)
            for dr in range(2):
                cp("vector", GC[rows, :, dr * 8:(dr + 1) * 8],
                   ps[rows, par, dr * 128:(dr + 1) * 128].rearrange("p (t h) -> p t h", h=8), r=[("ps", par)], w=["GC"])
            cp("vector", GTall[:, par], ps[:, 2 + par, 0:256].rearrange("p (t h) -> p t h", h=16), r=[("ps", 2 + par)], w=["GTall"])
        act(EG, GC, AF.Exp, r=["GC"], w=["EG"])
        for par in range(2):
            rows = slice(par * 64, (par + 1) * 64)
            tt("vector", EDEC[rows], GTall[rows, par], GC[rows], ALU.subtract, r=["GTall", "GC"], w=["EDEC"])
        act(EDEC, EDEC, AF.Exp, r=["EDEC"], w=["EDEC"])
        gt6 = GTall.rearrange("p a t (dp o) -> p a t dp o", o=2)
        for odd in range(2):
            rows = slice(odd * 64, (odd + 1) * 64)
            act(GLs[rows], gt6[rows, :, :, :, odd], AF.Exp, r=["GTall"], w=["GLs"])
        P.barrier()
        if stop == 2:
            for nm_, ap_ in [("kT", kT), ("qT", qT), ("k_tok", k_tok), ("v_tok", v_tok), ("sz", sz), ("gu", gu), ("graw", graw),
                             ("BETA", BETA), ("GC", GC), ("EG", EG), ("EDEC", EDEC), ("GTall", GTall), ("GLs", GLs)]:
                dump(nm_, ap_)
            return

        Arena.ptr = MARK
        o_buf = V([128, 16, 512], BF)
        o32 = V([128, 8, 64], F32)
        p25a = V([128, 8, 64], F32)
        p25b = V([128, 512], BF)
        ss8 = V([128, 16], F32)
        mark2 = Arena.ptr
        region = {"use_h": False}
        Hptr = [H_off]

        def V2(shape, dtype):
            n = 1
            for s_ in shape[1:]:
                n *= s_
            nb = (n * (4 if dtype == F32 else 2) + 31) // 32 * 32
            if Arena.ptr + nb <= NBYTES:
                return V(shape, dtype)
            save = Arena.ptr
            Arena.ptr = Hptr[0]
            v = V(shape, dtype)
            Hptr[0] = Arena.ptr
            assert Hptr[0] <= H_off + 32768, "H overflow"
            Arena.ptr = save
            return v

        SH = [128, 8, 64]
        attnT = [[V2(SH, BF) for _ in range(2)] for _ in range(2)]
        Rw = [[V2(SH, BF) for _ in range(2)] for _ in range(2)]
        Ub = [[V2(SH, F32) for _ in range(2)] for _ in range(2)]
        kdec = [[V2(SH, BF) for _ in range(2)] for _ in range(2)]
        Dg = V2(SH, F32)
        dff = V2(SH, F32)
        dinc = V2(SH, BF)
        dstr = V2(SH, BF)
        tmpb = V2(SH, BF)
        nmg = V2(SH, F32)
        PXr = [V2([128, 8, 192], BF) for _ in range(2)]
        Pbd = [PXr[i][:, :, 0:128] for i in range(2)]
        Xbf = [PXr[i][:, :, 128:192] for i in range(2)]
        PTbd = [V2([128, 8, 128], BF) for _ in range(2)]
        X32 = [V2(SH, F32) for _ in range(2)]
        tmpx = V2([128, 8, 64], F32)
        ks_sb = [V2(SH, BF) for _ in range(2)]
        qs_sb = [V2(SH, F32) for _ in range(2)]
        vnew = [V2(SH, BF) for _ in range(2)]
        otmp = [V2(SH, F32) for _ in range(2)]
        Sg = [V2([128, 4, 128], F32) for _ in range(2)]
        Sst = [V2([128, 4, 128], BF) for _ in range(2)]

        memset("gpsimd", o_buf, 0.0, w=["o_buf"])
        for i_ in range(2):
            memset("gpsimd", PXr[i_], 0.0, w=[("P", i_, 0), ("P", i_, 1), ("Xbf", i_, 0), ("Xbf", i_, 1)])
        for dr in range(2):
            memset("vector", Sst[dr], 0.0, w=[("S", dr)])

        HALF = [slice(0, 64), slice(64, 128)]

        def v4(ap3, odd):
            return ap3.rearrange("p (pr o) i -> p pr o i", o=2)[:, :, odd, :]

        def s4(ap2, odd):
            return ap2.rearrange("p (pr o) -> p pr o", o=2)[:, :, odd]

        def pv(rows, bank):
            return ps[rows, bank, :].rearrange("p (h i) -> p h i", i=64)

        def prep(t, dr):
            lim = stop - 300 if 300 <= stop < 400 else 99
            sl_ = t % 2
            hd0 = dr * 8
            tok0 = t * 128
            K = lambda nm: (nm, dr, sl_)
            for h in range(8):
                hr = (h % 2) * 64
                pr = h // 2
                bk = h % 2
                for par in range(2):
                    rows = HALF[par]
                    tk = slice(tok0 + par * 64, tok0 + (par + 1) * 64)
                    mm(ps[rows, bk, pr * 128:(pr + 1) * 128], kT[hr:hr + 64, pr, tk], kq[hr:hr + 64, pr, :, tk], r=["kT", "qT"], w=[("ps", bk)])
            psG = lambda odd, kind: ps[:, odd, :].rearrange("p (pr k i) -> p pr k i", k=2, i=64)[:, :, kind, :]
            if lim <= 1:
                return
            gcs = GC[:, t, hd0:hd0 + 8]
            tt("vector", Dg, bc_mid(ident64, 8), bc_last(gcs, 64), ALU.mult, r=["ident64", "GC"], w=["Dg"])
            tt("vector", nmg, bc_mid(NM[:, dr, :], 8), bc_last(gcs, 64), ALU.subtract, r=["NM", "GC"], w=["nmg"])
            tt("vector", dstr, bc_mid(offdiag, 8), bc_last(NBETA[:, t, hd0:hd0 + 8], 64), ALU.mult, r=["offdiag", "NBETA"], w=["dstr"])
            for par in range(2):
                rows = HALF[par]
                mm(ps[rows, 2 + par, :], ones128[rows, 0:64], Dg[rows].rearrange("p h i -> p (h i)"), r=["ones128", "Dg"], w=[("ps", 2 + par)])
            for par in range(2):
                rows = HALF[par]
                tt("vector", dff[rows], pv(rows, 2 + par), nmg[rows], ALU.add, r=[("ps", 2 + par), "nmg"], w=["dff"])
            act(dinc, dff, AF.Exp, r=["dff"], w=["dinc"])
            tt("vector", tmpb, dinc, dstr, ALU.mult, r=["dinc", "dstr"], w=["tmpb"])
            if lim <= 4:
                return
            for odd in range(2):
                for par in range(2):
                    rows = HALF[par]
                    tt("vector", v4(Pbd[0][rows, :, par * 64:(par + 1) * 64], odd), psG(odd, 0)[rows], v4(tmpb[rows], odd), ALU.mult,
                       r=[("ps", odd), "tmpb"], w=[("P", 0, 0), ("P", 0, 1)])
                tt("vector", v4(attnT[dr][sl_], odd), psG(odd, 1), v4(dinc, odd), ALU.mult, r=[("ps", odd), "dinc"], w=[K("attnT")])
            tt("vector", X32[0], bc_mid(ident64, 8), bc_last(BETA[:, t, hd0:hd0 + 8], 64), ALU.mult,
               r=["ident64", "BETA"], w=[("X32", 0, 0), ("X32", 0, 1)])
            cp("gpsimd", Xbf[0], X32[0], r=[("X32", 0, 0), ("X32", 0, 1)], w=[("Xbf", 0, 0), ("Xbf", 0, 1)])
            if lim <= 5:
                return
            HK = [0, 1]
            for h in range(8):
                mm(ps[:, h // 4, (h % 4) * 128:(h % 4 + 1) * 128], Pbd[0][:, h, :], ident_bf,
                   r=[("P", 0, h // 4), "ident_bf"], w=[("ps", h // 4)])
            for hf in range(2):
                cp("scalar", PTbd[0][:, hf * 4:(hf + 1) * 4, :],
                   ps[:, hf, :].rearrange("p (s c) -> p s c", c=128), r=[("ps", hf)], w=[("PT", 0, hf)])
            tick()
            for k in range(6):
                a, b_ = k % 2, (k + 1) % 2
                for h in range(8):
                    mm(ps[:, 4, h * 64:(h + 1) * 64], PTbd[a][:, h, :], Xbf[a][:, h, :],
                       r=[("PT", a, h // 4), ("Xbf", a, 0), ("Xbf", a, 1)], w=[("ps", 4)])
                tt("vector", Xbf[b_], ps[:, 4, :].rearrange("p (h i) -> p h i", i=64), X32[a], ALU.add,
                   r=[("X32", a, 0), ("X32", a, 1), ("ps", 4)], w=[("Xbf", b_, 0), ("Xbf", b_, 1)])
                if k < 5:
                    cp("scalar", tmpx, ps[:, 4, :].rearrange("p (h i) -> p h i", i=64), r=[("ps", 4)], w=["tmpx"])
                if k < 5:
                    for h in range(8):
                        mm(ps[:, h // 4, (h % 4) * 128:(h % 4 + 1) * 128], Pbd[a][:, h, :], PTbd[a][:, h, :],
                           r=[("P", a, h // 4), ("PT", a, h // 4)], w=[("ps", h // 4)])
                    for hf in range(2):
                        cp("scalar", PTbd[b_][:, hf * 4:(hf + 1) * 4, :],
                           ps[:, hf, :].rearrange("p (s c) -> p s c", c=128), r=[("ps", hf)], w=[("PT", b_, hf)])
                tick()
                if k < 4:
                    for h in range(8):
                        mm(ps[:, 2 + h // 4, (h % 4) * 128:(h % 4 + 1) * 128], PTbd[a][:, h, :], Pbd[a][:, h, :],
                           r=[("P", a, h // 4), ("PT", a, h // 4)], w=[("ps", 2 + h // 4)])
                    for hf in range(2):
                        cp("scalar", Pbd[b_][:, hf * 4:(hf + 1) * 4, :],
                           ps[:, 2 + hf, :].rearrange("p (s c) -> p s c", c=128), r=[("ps", 2 + hf)], w=[("P", b_, hf)])
                if k < 5:
                    tt("gpsimd", X32[b_], tmpx, X32[a], ALU.add, r=[("X32", a, 0), ("X32", a, 1), "tmpx"],
                       w=[("X32", b_, 0), ("X32", b_, 1)])
            XK = [("Xbf", 0, 0), ("Xbf", 0, 1)]
            tt("gpsimd", Rw[dr][sl_], Xbf[0], bc_last(EG[:, t, hd0:hd0 + 8], 64), ALU.mult, r=XK + ["EG"], w=[K("Rw")])
            for par in range(2):
                rows = HALF[par]
                for h in range(8):
                    mm(ps[rows, 2 + par, h * 64:(h + 1) * 64], Xbf[0][rows, h, :], v_tok[rows, t, h * 64:(h + 1) * 64],
                       r=XK + ["v_tok"], w=[("ps", 2 + par)])
            for par in range(2):
                rows = HALF[par]
                cp("scalar" if par == 0 else "vector", Ub[dr][sl_][rows], pv(rows, 2 + par), r=[("ps", 2 + par)], w=[K("Ub")])
            tt("gpsimd", kdec[dr][sl_], k_tok[:, t, :].rearrange("p (h d) -> p h d", d=64),
               bc_last(EDEC[:, t, hd0:hd0 + 8], 64), ALU.mult, r=["k_tok", "EDEC"], w=[K("kdec")])

        pending = []

        def tick():
            for g in list(pending):
                try:
                    next(g)
                except StopIteration:
                    pending.remove(g)

        def scan_dir(dr, chunks):
            for c in chunks:
                yield from scan(c, dr)

        def scan(c, dr):
            t = c // 2
            par = c % 2
            sl_ = t % 2
            hd0 = dr * 8
            rows = HALF[par]
            tk = slice(c * 64, (c + 1) * 64)
            K = lambda nm: (nm, dr, sl_)
            bW = 5
            bS = 5
            for pr in range(4):
                mm(ps[rows, 6, pr * 128:(pr + 1) * 128], kT[:, pr, tk], Sst[dr][:, pr, :], r=["kT", ("S", dr)], w=[("ps", 6)])
            cp("scalar", ks_sb[dr][rows], pv(rows, 6), r=[("ps", 6)], w=[("ks", dr)])
            yield
            for pr in range(4):
                mm(ps[rows, 7, pr * 128:(pr + 1) * 128], qT[:, pr, tk], Sst[dr][:, pr, :], r=["qT", ("S", dr)], w=[("ps", 7)])
            tt("vector", qs_sb[dr][rows], pv(rows, 7), bc_last(EG[rows, t, hd0:hd0 + 8], 64), ALU.mult,
               r=[("ps", 7), "EG"], w=[("qs", dr)])
            yield
            for h in range(8):
                mm(ps[rows, bW, h * 64:(h + 1) * 64], Rw[dr][sl_][rows, h, :], ks_sb[dr][rows, h, :], r=[K("Rw"), ("ks", dr)], w=[("ps", bW)])
            tt("vector", vnew[dr][rows], Ub[dr][sl_][rows], pv(rows, bW), ALU.subtract, r=[K("Ub"), ("ps", bW)], w=[("vnew", dr)])
            yield
            for h in range(8):
                mm(ps[rows, bW, h * 64:(h + 1) * 64], attnT[dr][sl_][rows, h, :], vnew[dr][rows, h, :], r=[K("attnT"), ("vnew", dr)], w=[("ps", bW)])
            tt("vector", otmp[dr][rows], pv(rows, bW), qs_sb[dr][rows], ALU.add, r=[("ps", bW), ("qs", dr)], w=[("otmp", dr)])
            yield
            for pr in range(4):
                mm(ps[:, bS, pr * 128:(pr + 1) * 128],
                   kdec[dr][sl_][rows, 2 * pr:2 * pr + 2, :].rearrange("p a d -> p (a d)"),
                   vnew[dr][rows, 2 * pr:2 * pr + 2, :].rearrange("p a e -> p (a e)"),
                   r=[K("kdec"), ("vnew", dr)], w=[("ps", bS)])
            ob = o_buf[rows, t, :].rearrange("p (h d) -> p h d", d=64)
            tt("gpsimd", ob, ob, otmp[dr][rows], ALU.add, r=["o_buf", ("otmp", dr)], w=["o_buf"])
            tt("gpsimd", Sg[dr], Sst[dr], bc_last(GLs[:, par, t, dr * 4:(dr + 1) * 4], 128), ALU.mult, r=[("S", dr), "GLs"], w=[("Sg", dr)])
            psS = ps[:, bS, :].rearrange("p (a e) -> p a e", e=128)
            for hh in range(2):
                rh = HALF[hh]
                cs = slice(hh * 64, (hh + 1) * 64)
                tt("vector", Sst[dr][rh, :, cs], Sg[dr][rh, :, cs], psS[rh, :, cs], ALU.add,
                   r=[("Sg", dr), ("ps", bS)], w=[("S", dr)])
            yield

        prep(0, 0)
        if stop == 30 or 300 <= stop < 400:
            P.barrier()
            for nm_, ap_ in [("attnT", attnT[0][0]), ("Rw", Rw[0][0]), ("Ub", Ub[0][0]), ("kdec", kdec[0][0]), ("Rb", Xbf[0]),
                             ("dinc", dinc), ("dstr", dstr), ("dff", dff), ("Dg", Dg)]:
                dump(nm_, ap_)
            return
        prep(15, 1)
        for rnd in range(16):
            pending.append(scan_dir(0, [2 * rnd, 2 * rnd + 1]))
            pending.append(scan_dir(1, [31 - 2 * rnd, 30 - 2 * rnd]))
            tick()
            if rnd < 15:
                prep(rnd + 1, 0)
                prep(14 - rnd, 1)
            while pending:
                tick()
        P.barrier()
        if stop == 3:
            dump("o_buf", o_buf)
            return

        mixT = H
        Arena.ptr = H_off + 32768
        x1 = V([128, NT, D], F32)
        for t in range(NT):
            dma("gpsimd", x1[:, t, :], xv[t], w=[("x1", t)])
        Arena.ptr = mark2
        o32s = [V([128, 8, 64], F32) for _ in range(2)]
        p25as = [V([128, 8, 64], F32) for _ in range(2)]
        p25bs = [V([128, 512], BF) for _ in range(2)]
        ss8s = [V([128, 16], F32) for _ in range(2)]
        def p25_A(t):
            b = t % 2
            o32, p25a, p25b, ss8 = o32s[b], p25as[b], p25bs[b], ss8s[b]
            ov = o_buf[:, t, :].rearrange("p (h d) -> p h d", d=64)
            act(o32, ov, AF.Square, r=["o_buf"], w=[("o32", b)])
            red(ss8[:, 0:8], o32, r=[("o32", b)], w=["gn%d_ss" % b])
            rstd_from(ss8[:, 0:8], ss8[:, 8:16], 1.0 / 64, "gn%d" % b)
            tt("vector", p25a, ov, bc_last(ss8[:, 8:16], 64), ALU.mult, r=["o_buf", "gn%d_r" % b], w=[("p25a", b)])
            tt("vector", p25b, p25a.rearrange("p h d -> p (h d)"), sz[:, t, :], ALU.mult, r=[("p25a", b), "sz"], w=[("p25b", b)])

        def p25_B(t):
            b = t % 2
            p25b = p25bs[b]
            pb = 4 + (t % 2)
            for c in range(4):
                tr(psb(pb)[:, c * 128:(c + 1) * 128], p25b[:, c * 128:(c + 1) * 128], ident_bf, r=[("p25b", b), "ident_bf"], w=[("ps", pb)])
            for c in range(4):
                tr(psb(pb)[:, (4 + c) * 128:(5 + c) * 128], gu[:, t, c * 128:(c + 1) * 128], ident_bf,
                   r=[("gu", t), "ident_bf"], w=[("ps", pb)])
            cp("scalar", mixT[:, :, t * 128:(t + 1) * 128], psb(pb).rearrange("p (c k) -> p c k", k=128),
               r=[("ps", pb)], w=[("mixT", t // 4)])

        p25_A(0)
        for t in range(NT):
            if t + 1 < NT:
                p25_A(t + 1)
            p25_B(t)
        P.barrier()
        if stop == 4:
            dump("mixT", mixT)
            return

        Arena.ptr = H_off + 32768 + NT * D * 4
        ring = [V([128, 8, 512], BF) for _ in range(2)]
        stage = [V([128, 8, 512], F32) for _ in range(2)]
        ffT = V([128, 4, S], BF)
        rt = [V([128, 512], BF) for _ in range(2)]
        xn2 = [V([128, D], BF) for _ in range(2)]
        junk3 = [V([128, D], BF) for _ in range(2)]
        ss3b = V([128, 4], F32)
        outt = [V([128, D], F32) for _ in range(2)]
        gvec = V([128, D], F32)
        ss3 = V([128, 4], F32)
        h2T = H

        dma("sync", gvec, norm_ffn.to_broadcast([128, D]), w=["gvec"])
        wov = w_out.rearrange("(c p) n -> p c n", p=128)
        w1v = w_ff1.rearrange("(c p) n -> p c n", p=128)
        w2v = w_ff2.rearrange("(c p) n -> p c n", p=128)
        yv = y.rearrange("(t p) d -> t p d", p=128)
        rcount = [0]

        def stream(src_ap, as_w2=False):
            sl_ = rcount[0] % 2
            rcount[0] += 1
            dst = stage[sl_]
            if as_w2:
                dst = dst.rearrange("p a b -> p (a b)").rearrange("p (f n) -> p f n", n=D)
            dma("sync", dst, src_ap, w=[("stage", sl_)])
            cp("vector", ring[sl_], stage[sl_], r=[("stage", sl_)], w=[("ring", sl_)])
            return sl_

        wo_slots = [stream(wov[:, :, hf * 512:(hf + 1) * 512]) for hf in range(2)]
        def p3a_A(t):
            for nh in range(2):
                for ct in range(8):
                    mm(ps[:, nh, :], mixT[:, ct, t * 128:(t + 1) * 128], ring[wo_slots[nh]][:, ct, :],
                       start=(ct == 0), stop=(ct == 7), r=[("HT", t), ("ring", wo_slots[nh])], w=[("ps", nh)])
                tt("vector", x1[:, t, nh * 512:(nh + 1) * 512], ps[:, nh, :], x1[:, t, nh * 512:(nh + 1) * 512], ALU.add,
                   r=[("ps", nh), ("x1", t)], w=[("x1", t)])
            b = t % 2
            sst = ss3 if b == 0 else ss3b
            act(junk3[b], x1[:, t, :], AF.Square, r=[("x1", t)], w=[("junk3", b), "n2%d_ss" % b], accum_out=sst[:, 0:1])
            rstd_from(sst[:, 0:1], sst[:, 1:2], 1.0 / D, "n2%d" % b)
            stt("vector", xn2[b], x1[:, t, :], sst[:, 1:2], gvec, ALU.mult, ALU.mult, r=[("x1", t), "n2%d_r" % b, "gvec"], w=[("xn2", b)])

        def p3a_B(t):
            b = t % 2
            pb = 2 + (t % 2)
            for c in range(8):
                tr(psb(pb)[:, c * 128:(c + 1) * 128], xn2[b][:, c * 128:(c + 1) * 128], ident_bf, r=[("xn2", b), "ident_bf"], w=[("ps", pb)])
            cp("scalar", h2T[:, :, t * 128:(t + 1) * 128], psb(pb).rearrange("p (c k) -> p c k", k=128),
               r=[("ps", pb)], w=[("HT", t)])

        p3a_A(0)
        for t in range(NT):
            if t + 1 < NT:
                p3a_A(t + 1)
            p3a_B(t)
        dma("sync", gvec, norm_final.to_broadcast([128, D]), r=["gvec"], w=["gvec"])
        def final_tile(t):
            ob_ = t % 2
            sst = ss3 if ob_ == 0 else ss3b
            act(junk3[ob_], x1[:, t, :], AF.Square, r=[("x1", t)], w=[("junk3", ob_), "n3%d_ss" % ob_], accum_out=sst[:, 2:3])
            rstd_from(sst[:, 2:3], sst[:, 3:4], 1.0 / D, "n3%d" % ob_)
            stt("vector", outt[ob_], x1[:, t, :], sst[:, 3:4], gvec, ALU.mult, ALU.mult,
                r=[("x1", t), "n3%d_r" % ob_, "gvec"], w=[("outt", ob_)])
            dma("gpsimd" if t % 2 == 0 else "sync", yv[t], outt[ob_], r=[("outt", ob_)])


        HTK = [("HT", t) for t in range(NT)]
        ecount = 0
        for g in range(8):
            s1 = stream(w1v[:, :, g * 512:(g + 1) * 512])
            for f4 in range(4):
                for sp in range(4):
                    pb = 4 + (ecount % 4)
                    for kc in range(8):
                        mm(ps[:, pb, :], ring[s1][:, kc, f4 * 128:(f4 + 1) * 128], h2T[:, kc, sp * 512:(sp + 1) * 512],
                           start=(kc == 0), stop=(kc == 7), r=[("ring", s1)] + HTK[sp * 4:(sp + 1) * 4], w=[("ps", pb)])
                    rb = ecount % 2
                    act(rt[rb], ps[:, pb, :], AF.Relu, r=[("ps", pb)], w=[("rt", rb)])
                    tt("vector" if rb == 0 else "gpsimd", ffT[:, f4, sp * 512:(sp + 1) * 512], rt[rb], rt[rb], ALU.mult,
                       r=[("rt", rb)], w=[("ffT", f4, sp)])
                    ecount += 1
            s2 = stream(w2v[:, g * 4:(g + 1) * 4, :], as_w2=True)
            w2s = ring[s2].rearrange("p a b -> p (a b)").rearrange("p (f n) -> p f n", n=D)
            for t in range(NT):
                for nh in range(2):
                    pb = (t * 2 + nh) % 4
                    for f4 in range(4):
                        mm(ps[:, pb, :], ffT[:, f4, t * 128:(t + 1) * 128], w2s[:, f4, nh * 512:(nh + 1) * 512],
                           start=(f4 == 0), stop=(f4 == 3), r=[("ffT", f4, t // 4), ("ring", s2)], w=[("ps", pb)])
                    tt("vector", x1[:, t, nh * 512:(nh + 1) * 512], ps[:, pb, :], x1[:, t, nh * 512:(nh + 1) * 512], ALU.add,
                       r=[("ps", pb), ("x1", t)], w=[("x1", t)])
                if g == 7 and t >= 1:
                    final_tile(t - 1)
            if g == 7:
                final_tile(NT - 1)

    phases()

    import contextlib
    with contextlib.ExitStack() as st:
        sems = {e: st.enter_context(nc.semaphore("s_" + e)) for e in Prog.ENG}
        dsems = {}
        for e in ("sync", "gpsimd"):
            for i in range(NDS):
                dsems[(e, i)] = st.enter_context(nc.semaphore("d_%s_%d" % (e, i)))
        P.emit(nc, sems, dsems)
    nc._dumps = dumps
    return nc


_NC = None


def kernel(**inputs):
    global _NC
    if _NC is None:
        _NC = build()
    nc = _NC
    names = ["norm_mix", "w_in", "conv_w", "a_log_fwd", "dt_bias_fwd", "a_log_bwd", "dt_bias_bwd", "gdn_norm",
             "sgu_v_norm", "sgu_w", "sgu_b", "sgu_out_norm", "w_out", "norm_ffn", "w_ff1", "w_ff2", "norm_final"]
    shp = {"norm_mix": (1, D), "w_in": (D, INC), "conv_w": (5, 1536), "a_log_fwd": (1, 8), "dt_bias_fwd": (1, 8),
           "a_log_bwd": (1, 8), "dt_bias_bwd": (1, 8), "gdn_norm": (1, 64), "sgu_v_norm": (1, 512),
           "sgu_w": (4, 128, 128), "sgu_b": (4, 128), "sgu_out_norm": (1, 512), "w_out": (D, D), "norm_ffn": (1, D),
           "w_ff1": (D, FF), "w_ff2": (FF, D), "norm_final": (1, D)}
    shared = {n: np.ascontiguousarray(np.asarray(inputs[n], dtype=np.float32).reshape(shp[n])) for n in names}
    xin = np.asarray(inputs["x"], dtype=np.float32)
    in_maps = []
    for c in range(8):
        m = dict(shared)
        m["x"] = np.ascontiguousarray(xin[c])
        in_maps.append(m)
    res = run_bass_kernel_spmd(nc, in_maps, core_ids=list(range(8)))
    return np.stack([np.asarray(res.results[c]["y"], dtype=np.float32) for c in range(8)], axis=0)
```
